# Optimizing a Trainium2 kernel written in Bass

```python
import jax, jax.numpy as jnp
from jax import lax
import numpy as np

D_MODEL = 2048
BATCH = 4
SEQ = 2048
DEPTH = 1
DEC_BATCH = 8
DEC_SEQ = 32
PAST_LEN = 2048

CHUNK = 64
MIX_WIDTH = D_MODEL
SB_WIDTH = MIX_WIDTH // 2
SB_HEAD_DIM = 128
SB_HEADS = SB_WIDTH // SB_HEAD_DIM
POOL_WIDTH = MIX_WIDTH - SB_WIDTH
POOL_WINDOWS = (2, 4, 8, 16)
POOL_GROUPS = len(POOL_WINDOWS)
POOL_GROUP_DIM = POOL_WIDTH // POOL_GROUPS
POOL_STATE = max(POOL_WINDOWS) - 1
IN_WIDTH = 3 * SB_WIDTH + POOL_WIDTH
D_FF = 5632
PLE_DIM = 256
Q_BLOCK = 128
RMS_EPS = 1e-6
SB_SCALE = SB_HEAD_DIM ** -0.5

kernel_name = "hymba_stickbreak_pool_macaron_stream_step"


def _rmsnorm(x, g):
    xf = x.astype(jnp.float32)
    r = lax.rsqrt(jnp.mean(xf * xf, axis=-1, keepdims=True) + RMS_EPS)
    return (xf * r * g.astype(jnp.float32)).astype(x.dtype)


def _swiglu(x, w_gate, w_up, w_down):
    return (jax.nn.silu(x @ w_gate) * (x @ w_up)) @ w_down


def _sb_block(qb, qpos, k, v, kpos):
    z = jnp.einsum('bqhd,bkhd->bhqk', qb, k).astype(jnp.float32) * SB_SCALE
    mask = kpos[None, :] < qpos[:, None]
    log_beta = jax.nn.log_sigmoid(z)
    log_rest = jnp.where(mask, log_beta - z, 0.0)
    suffix = lax.cumsum(log_rest, axis=3, reverse=True) - log_rest
    a = jnp.where(mask, jnp.exp(log_beta + suffix), 0.0)
    return jnp.einsum('bhqk,bkhd->bqhd', a, v.astype(jnp.float32))


def _stick_breaking(q, k, v, qpos, kpos):
    b, t, h, d = q.shape
    if t <= Q_BLOCK:
        return _sb_block(q, qpos, k, v, kpos)
    nb = t // Q_BLOCK
    qb = q.reshape(b, nb, Q_BLOCK, h, d).transpose(1, 0, 2, 3, 4)
    pb = qpos.reshape(nb, Q_BLOCK)
    out = lax.map(lambda args: _sb_block(args[0], args[1], k, v, kpos), (qb, pb))
    return out.transpose(1, 0, 2, 3, 4).reshape(b, t, h, d)


def _multiscale_pool(xp, pool_past, pos, w_pool):
    b, t, _ = xp.shape
    full = jnp.concatenate([pool_past.astype(xp.dtype), xp], axis=1)
    ff = full.astype(jnp.float32)
    csum = jnp.concatenate([jnp.zeros((b, 1, POOL_WIDTH), jnp.float32), jnp.cumsum(ff, axis=1)], axis=1)
    xf = xp.astype(jnp.float32)
    outs = []
    for g, w in enumerate(POOL_WINDOWS):
        sl = slice(g * POOL_GROUP_DIM, (g + 1) * POOL_GROUP_DIM)
        cg = csum[:, :, sl]
        win = cg[:, POOL_STATE + 1:POOL_STATE + 1 + t] - cg[:, POOL_STATE + 1 - w:POOL_STATE + 1 - w + t]
        cnt = jnp.minimum(pos + 1, w).astype(jnp.float32)
        outs.append(win / cnt[None, :, None] - xf[:, :, sl])
    pooled = jnp.stack(outs, axis=2)
    mixed = jnp.einsum('btgc,gcd->btgd', pooled, w_pool.astype(jnp.float32))
    return mixed.reshape(b, t, POOL_WIDTH).astype(xp.dtype), full[:, -POOL_STATE:]


def _layer(x, p, pos, k_past, v_past, pool_past, g_ffn1, w1_gate, w1_up, w1_down, g_mix, w_in,
           g_attn_out, w_pool, pool_scale, w_out, g_ffn2, w2_gate, w2_up, w2_down,
           g_ple, w_ple_gate, w_ple_proj):
    b, t, _ = x.shape
    h = x + 0.5 * _swiglu(_rmsnorm(x, g_ffn1), w1_gate, w1_up, w1_down)
    u = _rmsnorm(h, g_mix)
    z = u @ w_in
    q, k, v, xp = jnp.split(z, [SB_WIDTH, 2 * SB_WIDTH, 3 * SB_WIDTH], axis=-1)
    q = q.reshape(b, t, SB_HEADS, SB_HEAD_DIM)
    k = k.reshape(b, t, SB_HEADS, SB_HEAD_DIM)
    v = v.reshape(b, t, SB_HEADS, SB_HEAD_DIM)
    if k_past is None:
        k_all, v_all = k, v
    else:
        k_all = jnp.concatenate([k_past.astype(k.dtype), k], axis=1)
        v_all = jnp.concatenate([v_past.astype(v.dtype), v], axis=1)
    kpos = jnp.arange(k_all.shape[1], dtype=jnp.int32)
    attn = _stick_breaking(q, k_all, v_all, pos, kpos).reshape(b, t, SB_WIDTH)
    attn = _rmsnorm(attn.astype(x.dtype), g_attn_out)
    if pool_past is None:
        pool_past = jnp.zeros((b, POOL_STATE, POOL_WIDTH), xp.dtype)
    pooled, new_pool = _multiscale_pool(xp, pool_past, pos, w_pool)
    pooled = _rmsnorm(pooled, pool_scale)
    h = h + jnp.concatenate([attn, pooled], axis=-1) @ w_out
    h = h + 0.5 * _swiglu(_rmsnorm(h, g_ffn2), w2_gate, w2_up, w2_down)
    gate = jax.nn.sigmoid((_rmsnorm(h, g_ple) @ w_ple_gate).astype(jnp.float32))
    h = h + (gate * (p @ w_ple_proj).astype(jnp.float32)).astype(h.dtype)
    return h, k, v, new_pool


def setup_inputs(seed: int = 0) -> dict:
    key = jax.random.key(seed)
    ks = jax.random.split(key, 28)
    f32 = jnp.float32

    def nrm(k, shape, scale):
        return jax.random.normal(k, shape, f32) * scale

    def gain(k, shape):
        return 1.0 + 0.1 * jax.random.normal(k, shape, f32)

    D = D_MODEL
    return {
        "x_prompt": nrm(ks[0], (BATCH, SEQ, D), 1.0),
        "x_sample": nrm(ks[1], (DEC_BATCH, DEC_SEQ, D), 1.0),
        "cache_k": nrm(ks[2], (DEPTH, DEC_BATCH, PAST_LEN, SB_HEADS, SB_HEAD_DIM), 1.0),
        "cache_v": nrm(ks[3], (DEPTH, DEC_BATCH, PAST_LEN, SB_HEADS, SB_HEAD_DIM), 1.0),
        "state_pool": nrm(ks[4], (DEPTH, DEC_BATCH, POOL_STATE, POOL_WIDTH), 1.0),
        "p_prompt": nrm(ks[5], (DEPTH, BATCH, SEQ, PLE_DIM), 1.0),
        "p_sample": nrm(ks[6], (DEPTH, DEC_BATCH, DEC_SEQ, PLE_DIM), 1.0),
        "g_ffn1": gain(ks[7], (DEPTH, D)),
        "w1_gate": nrm(ks[8], (DEPTH, D, D_FF), D ** -0.5),
        "w1_up": nrm(ks[9], (DEPTH, D, D_FF), D ** -0.5),
        "w1_down": nrm(ks[10], (DEPTH, D_FF, D), D_FF ** -0.5),
        "g_mix": gain(ks[11], (DEPTH, D)),
        "w_in": nrm(ks[12], (DEPTH, D, IN_WIDTH), D ** -0.5),
        "g_attn_out": gain(ks[13], (DEPTH, SB_WIDTH)),
        "w_pool": nrm(ks[14], (DEPTH, POOL_GROUPS, POOL_GROUP_DIM, POOL_GROUP_DIM), POOL_GROUP_DIM ** -0.5),
        "pool_scale": gain(ks[15], (DEPTH, POOL_WIDTH)),
        "w_out": nrm(ks[16], (DEPTH, MIX_WIDTH, D), MIX_WIDTH ** -0.5),
        "g_ffn2": gain(ks[17], (DEPTH, D)),
        "w2_gate": nrm(ks[18], (DEPTH, D, D_FF), D ** -0.5),
        "w2_up": nrm(ks[19], (DEPTH, D, D_FF), D ** -0.5),
        "w2_down": nrm(ks[20], (DEPTH, D_FF, D), D_FF ** -0.5),
        "g_ple": gain(ks[21], (DEPTH, D)),
        "w_ple_gate": nrm(ks[22], (DEPTH, D, D), D ** -0.5),
        "w_ple_proj": nrm(ks[23], (DEPTH, PLE_DIM, D), PLE_DIM ** -0.5),
        "g_final": gain(ks[24], (D,)),
    }


def reference(x_prompt, x_sample, cache_k, cache_v, state_pool, p_prompt, p_sample,
              g_ffn1, w1_gate, w1_up, w1_down, g_mix, w_in, g_attn_out, w_pool, pool_scale,
              w_out, g_ffn2, w2_gate, w2_up, w2_down, g_ple, w_ple_gate, w_ple_proj, g_final):
    t_p = x_prompt.shape[1]
    t_s = x_sample.shape[1]
    past = cache_k.shape[2]
    pos_p = jnp.arange(t_p, dtype=jnp.int32)
    pos_s = past + jnp.arange(t_s, dtype=jnp.int32)
    hp, hs = x_prompt, x_sample
    kp_l, vp_l, sp_l, ks_l, vs_l, ss_l = [], [], [], [], [], []
    for i in range(DEPTH):
        wts = (g_ffn1[i], w1_gate[i], w1_up[i], w1_down[i], g_mix[i], w_in[i], g_attn_out[i],
               w_pool[i], pool_scale[i], w_out[i], g_ffn2[i], w2_gate[i], w2_up[i], w2_down[i],
               g_ple[i], w_ple_gate[i], w_ple_proj[i])
        hp, kp, vp, sp = _layer(hp, p_prompt[i], pos_p, None, None, None, *wts)
        hs, ks_, vs_, ss = _layer(hs, p_sample[i], pos_s, cache_k[i], cache_v[i], state_pool[i], *wts)
        kp_l.append(kp); vp_l.append(vp); sp_l.append(sp)
        ks_l.append(ks_); vs_l.append(vs_); ss_l.append(ss)
    y_prompt = _rmsnorm(hp, g_final)
    y_sample = _rmsnorm(hs, g_final)
    new_k_prompt = jnp.stack(kp_l)
    new_v_prompt = jnp.stack(vp_l)
    new_pool_prompt = jnp.stack(sp_l)
    new_k_sample = jnp.stack(ks_l)
    new_v_sample = jnp.stack(vs_l)
    new_pool_sample = jnp.stack(ss_l)
    return (y_prompt, y_sample, new_k_prompt, new_v_prompt, new_pool_prompt, new_k_sample, new_v_sample, new_pool_sample)
```

```python
import numpy as np
from contextlib import ExitStack
import concourse.bass as bass
import concourse.mybir as mybir
from concourse.bass_utils import run_bass_kernel_spmd

F32 = mybir.dt.float32
BF16 = mybir.dt.bfloat16
AF = mybir.ActivationFunctionType
ALU = mybir.AluOpType

D = 2048
NC_ = 16
DFF = 5632
NG = 11
T_OWN = 1056
T_OTH = 1024
TLOC = 2080
PLE = 256
EPS = 1e-6
SB_SCALE = 128.0 ** -0.5
TILES_OWN = [(0, 512), (512, 512), (1024, 32)]
TILES_OTH = [(0, 512), (512, 512)]
TILES_FFN = [(0, 352), (352, 352), (704, 352)]
NDS = 28


class Tok:
    __slots__ = ("si", "val", "eng")

    def __init__(self, si, val, eng):
        self.si = si
        self.val = val
        self.eng = eng


class Buf:
    __slots__ = ("w", "r")

    def __init__(self):
        self.w = None
        self.r = {}


class TR:
    def __init__(self, nc, es):
        self.nc = nc
        self.sems = []
        self.eng = {}
        for name, h in (("pe", nc.tensor), ("act", nc.scalar), ("dve", nc.vector),
                        ("pool", nc.gpsimd), ("sp", nc.sync)):
            s = es.enter_context(nc.semaphore("sem_" + name))
            self.sems.append(s)
            self.eng[name] = dict(h=h, si=len(self.sems) - 1, cnt=0, waited={})
        self.dsq = {"sp": [], "pool": []}
        self.ds = []
        for q, n in (("sp", NDS), ("pool", 12)):
            for i in range(n):
                s = es.enter_context(nc.semaphore("dsem_%s%d" % (q, i)))
                self.sems.append(s)
                d = dict(si=len(self.sems) - 1, cnt=0)
                self.dsq[q].append(d)
                self.ds.append(d)
        self.rr = {"sp": 0, "pool": 0}

    def wait(self, en, tok):
        if tok is None:
            return
        if tok.eng == en and en == "pe":
            return
        e = self.eng[en]
        if e["waited"].get(tok.si, 0) >= tok.val:
            return
        e["h"].wait_ge(self.sems[tok.si], tok.val)
        e["waited"][tok.si] = tok.val

    def _deps(self, en, reads, writes):
        for b in reads:
            for t in (b.w if isinstance(b.w, list) else [b.w]):
                self.wait(en, t)
        for b in writes:
            for t in (b.w if isinstance(b.w, list) else [b.w]):
                self.wait(en, t)
            for t in b.r.values():
                self.wait(en, t)

    def _commit(self, tok, reads, writes):
        for b in reads:
            b.r[tok.si] = tok
        for b in writes:
            b.w = tok
            b.r = {}

    def op(self, en, fn, reads=(), writes=()):
        self._deps(en, reads, writes)
        e = self.eng[en]
        ins = fn(e["h"])
        e["cnt"] += 1
        ins.then_inc(self.sems[e["si"]], 1)
        tok = Tok(e["si"], e["cnt"], en)
        self._commit(tok, reads, writes)
        return tok

    def dma(self, en, out, in_, reads=(), writes=()):
        d = self.dsq[en][self.rr[en]]
        self.rr[en] = (self.rr[en] + 1) % len(self.dsq[en])
        if d["cnt"] > 0:
            self.wait(en, Tok(d["si"], d["cnt"], None))
        self._deps(en, reads, writes)
        ins = self.eng[en]["h"].dma_start(out=out, in_=in_)
        d["cnt"] += 16
        ins.then_inc(self.sems[d["si"]], 16)
        tok = Tok(d["si"], d["cnt"], None)
        self._commit(tok, reads, writes)
        return tok

    def barrier(self):
        for en in self.eng:
            for fn_, f in self.eng.items():
                if f["cnt"] > 0 and not (fn_ == en and en in ("pe", "sp")):
                    self.wait(en, Tok(f["si"], f["cnt"], fn_))
            for d in self.ds:
                if d["cnt"] > 0:
                    self.wait(en, Tok(d["si"], d["cnt"], None))

    def finish(self):
        self.barrier()


class _Stop(Exception):
    pass


def build_program(stop_at=None):
    nc = bass.Bass("TRN2", target_bir_lowering=False)
    es = ExitStack()
    E = es.enter_context

    def din(name, shape):
        return nc.dram_tensor(name, list(shape), F32, kind="ExternalInput").ap()

    def dout(name, shape):
        return nc.dram_tensor(name, list(shape), F32, kind="ExternalOutput").ap()

    x_own = din("x_own", (T_OWN, D))
    x_oth = din("x_oth", (T_OTH, D))
    p_own = din("p_own", (T_OWN, PLE))
    ck = din("ck", (2048, 1024))
    cv = din("cv", (2048, 1024))
    spool = din("spool", (15, 1024))
    w1g = din("w1g", (D, DFF)); w1u = din("w1u", (D, DFF)); w1d = din("w1d", (DFF, D))
    w2g = din("w2g", (D, DFF)); w2u = din("w2u", (D, DFF)); w2d = din("w2d", (DFF, D))
    w_in = din("w_in", (D, 4096))
    w_pool = din("w_pool", (4, 256, 256))
    w_out = din("w_out", (D, D))
    wpg = din("wpg", (D, D))
    wpp = din("wpp", (PLE, D))
    gains = din("gains", (128, 96))
    cmat = din("cmat", (128, 4, 128))
    dmask = din("dmask", (128, 4, 512))
    flagv = din("flagv", (128, 1))
    invc = din("invc", (128, 4, T_OWN))

    y_own = dout("y_own", (T_OWN, D))
    k_own = dout("k_own", (T_OWN, 1024))
    v_own = dout("v_own", (T_OWN, 1024))
    pool_p = dout("pool_p", (15, 1024))
    pool_s = dout("pool_s", (15, 1024))

    kT_scr = nc.dram_tensor("kT_scr", [8, 128, TLOC], BF16, kind="Internal").ap()
    v_scr = nc.dram_tensor("v_scr", [TLOC, 1024], BF16, kind="Internal").ap()

    tr = TR(nc, es)
    op, dma = tr.op, tr.dma

    def stop(n):
        if stop_at is not None and stop_at == n:
            raise _Stop()

    uid = [0]

    def sb(name, shape, dt, stack=None):
        uid[0] += 1
        return (stack or es).enter_context(nc.sbuf_tensor("%s_%d" % (name, uid[0]), list(shape), dt))

    hT = sb("hT", (128, NC_, T_OWN), F32)
    U = sb("U", (128, NC_, T_OWN), BF16)
    cm_f = sb("cm_f", (128, 4, 128), F32)
    cm_b = sb("cm_b", (128, 4, 128), BF16)
    gn = sb("gn", (128, 96), F32)
    flag = sb("flag", (128, 1), F32)
    rstd = sb("rstd", (128, 512), F32)
    sqt = [sb("sqt%d" % i, (128, 512), BF16) for i in range(3)]
    B_hT = {(c, i): Buf() for c in range(NC_) for i in range(3)}
    B_U = [Buf() for _ in range(3)]
    B_c = Buf()
    B_rstd = Buf()
    B_sq = [Buf() for _ in range(3)]
    B_kscr = Buf()
    B_vscr = Buf()

    banks = [E(nc.psum_tensor("pb%d" % i, [128, 512], F32)) for i in range(8)]
    B_pb = [Buf() for _ in range(8)]
    pb6_bf = banks[6].bitcast(BF16)

    ident_f = cm_f[:, 0, :]
    ident_b = cm_b[:, 0, :]
    ones_b = cm_b[:, 1, :]
    ntri_b = cm_b[:, 2, :]
    nones_b = cm_b[:, 3, :]

    dma("sp", cm_f[:], cmat, writes=[B_c])
    dma("sp", gn[:], gains, writes=[B_c])
    dma("sp", flag[:], flagv, writes=[B_c])
    dma("pool", cm_b[:], cmat, writes=[B_c])

    G_FFN1, G_MIX, G_FFN2, G_PLE, G_FIN, G_ATT, G_POOL = 0, 16, 32, 48, 64, 80, 88

    def load_xT(x_ap, T, stack):
        xin = [sb("xin%d" % i, (128, D), F32, stack) for i in range(3)]
        B_xin = [Buf(), Buf(), Buf()]
        nblk = (T + 127) // 128
        q = 0
        for b in range(nblk):
            nb = min(128, T - b * 128)
            xi, bx = xin[b % 3], B_xin[b % 3]
            dma("sp", xi[0:nb, :], x_ap[b * 128:b * 128 + nb, :], writes=[bx])
            for cg in range(4):
                pbi = q % 2
                q += 1

                def f(pe, cg=cg, nb=nb, xi=xi, pbi=pbi):
                    ins = None
                    for j in range(4):
                        c = cg * 4 + j
                        ins = pe.transpose(banks[pbi][:, j * 128:j * 128 + nb], xi[0:nb, c * 128:(c + 1) * 128],
                                           ident_f[0:nb, 0:nb])
                    return ins
                op("pe", f, reads=[bx, B_c], writes=[B_pb[pbi]])
                ti = min(b * 128 // 512, 2)
                src = banks[pbi][:].rearrange("p (j t) -> p j t", j=4)[:, :, 0:nb]
                dst = hT[:, cg * 4:cg * 4 + 4, b * 128:b * 128 + nb]
                eng = "act" if (q % 2) else "dve"
                if eng == "act":
                    op("act", lambda a, dst=dst, src=src: a.copy(dst, src), reads=[B_pb[pbi]],
                       writes=[B_hT[(cg * 4 + j, ti)] for j in range(4)])
                else:
                    op("dve", lambda v, dst=dst, src=src: v.tensor_copy(dst, src), reads=[B_pb[pbi]],
                       writes=[B_hT[(cg * 4 + j, ti)] for j in range(4)])

    def rmsnorm(src, srcbufs, C, goff, dst, dstbufs, tiles, width):
        for ti, (t0, n) in enumerate(tiles):
            for c in range(C):
                k = c % 3
                op("act", lambda a, c=c, k=k: a.activation(sqt[k][:, 0:n], src[:, c, t0:t0 + n], AF.Square),
                   reads=[srcbufs(c, ti)], writes=[B_sq[k]])
                op("pe", lambda pe, c=c, k=k: pe.matmul(banks[7][:, 0:n], ones_b, sqt[k][:, 0:n],
                                                        start=(c == 0), stop=(c == C - 1)),
                   reads=[B_sq[k], B_c], writes=[B_pb[7]])
            op("act", lambda a: a.activation(rstd[:, 0:n], banks[7][:, 0:n], AF.Sqrt, bias=EPS, scale=1.0 / width),
               reads=[B_pb[7]], writes=[B_rstd])
            op("dve", lambda v: v.reciprocal(rstd[:, 0:n], rstd[:, 0:n]), reads=[B_rstd], writes=[B_rstd])
            multi = {}
            for c in range(C):
                d = dstbufs(c, ti)
                tmp = Buf()
                tmp.w = d.w
                tmp.r = dict(d.r)
                tok = op("dve", lambda v, c=c: v.scalar_tensor_tensor(
                    out=dst(c, t0, n), in0=src[:, c, t0:t0 + n], scalar=gn[:, goff + c:goff + c + 1],
                    in1=rstd[:, 0:n], op0=ALU.mult, op1=ALU.mult),
                    reads=[srcbufs(c, ti), B_rstd, B_c], writes=[tmp])
                multi.setdefault(id(d), (d, []))[1].append(tok)
            for d, toks in multi.values():
                d.w = toks
                d.r = {}

    def hbuf(c, ti):
        return B_hT[(c, ti)]

    def norm_h_to_U(goff, tiles):
        rmsnorm(hT, hbuf, NC_, goff, lambda c, t0, n: U[:, c, t0:t0 + n], lambda c, ti: B_U[ti], tiles, D)

    class WPool:
        def __init__(self, name, n, shape, stack):
            self.t = [sb("%s%d" % (name, i), shape, BF16, stack) for i in range(n)]
            self.b = [Buf() for _ in range(n)]
            self.i = 0

        def load(self, src_ap):
            i = self.i
            self.i = (self.i + 1) % len(self.t)
            dma("pool", self.t[i][:], src_ap, writes=[self.b[i]])
            return self.t[i], self.b[i]

    def wcols(w_ap, c0, ncol):
        return w_ap.rearrange("(c p) f -> p c f", p=128)[:, :, c0:c0 + ncol]

    def ffn(wg, wu, wd, tiles):
        with ExitStack() as st:
            gu = WPool("gu", 4, (128, NC_, 256), st)
            wdp = WPool("wd", 2, (128, 4, D), st)
            actb = [sb("act%d" % i, (128, 4, T_OWN), BF16, st) for i in range(2)]
            B_act = [[[Buf() for _ in range(3)] for _ in range(4)] for _ in range(2)]
            sg = [sb("sg%d" % i, (128, 512), F32, st) for i in range(2)]
            B_sg = [Buf(), Buf()]
            cnt = {"gu": 0, "dn": 0, "sg": 0}

            def gate_up(g):
                par = g % 2
                for pr in range(2):
                    wgt, wgb = gu.load(wcols(wg, g * 512 + pr * 256, 256))
                    wut, wub = gu.load(wcols(wu, g * 512 + pr * 256, 256))
                    for ti, (t0, n) in enumerate(tiles):
                        for j in range(2):
                            fc = pr * 2 + j
                            gb = cnt["gu"] % 2
                            ub = 2 + cnt["gu"] % 2
                            cnt["gu"] += 1

                            def fg(pe, wt=wgt, bk=gb, j=j, t0=t0, n=n):
                                ins = None
                                for k in range(NC_):
                                    ins = pe.matmul(banks[bk][:, 0:n], wt[:, k, j * 128:(j + 1) * 128],
                                                    U[:, k, t0:t0 + n], start=(k == 0), stop=(k == NC_ - 1))
                                return ins
                            op("pe", fg, reads=[wgb, B_U[ti]], writes=[B_pb[gb]])
                            op("pe", lambda pe, wt=wut, bk=ub, j=j, t0=t0, n=n: fg(pe, wt, bk, j, t0, n),
                               reads=[wub, B_U[ti]], writes=[B_pb[ub]])
                            si = cnt["sg"] % 2
                            cnt["sg"] += 1
                            op("act", lambda a, si=si, gb=gb, n=n: a.activation(sg[si][:, 0:n], banks[gb][:, 0:n], AF.Silu),
                               reads=[B_pb[gb]], writes=[B_sg[si]])
                            op("dve", lambda v, si=si, ub=ub, fc=fc, t0=t0, n=n, par=par: v.tensor_tensor(
                                out=actb[par][:, fc, t0:t0 + n], in0=sg[si][:, 0:n], in1=banks[ub][:, 0:n], op=ALU.mult),
                               reads=[B_sg[si], B_pb[ub]], writes=[B_act[par][fc][ti]])

            def down(grp):
                wds = []
                for g in grp:
                    wds.append(wdp.load(wd[g * 512:(g + 1) * 512, :].rearrange("(c p) f -> p c f", p=128)) + (g % 2,))
                nmm = 4 * len(grp)
                for ti, (t0, n) in enumerate(tiles):
                    for dc in range(NC_):
                        bk = 4 + cnt["dn"] % 2
                        cnt["dn"] += 1

                        def fd(pe, bk=bk, dc=dc, t0=t0, n=n):
                            ins = None
                            q_ = 0
                            for (wdt, wdb, par) in wds:
                                for fc in range(4):
                                    ins = pe.matmul(banks[bk][:, 0:n], wdt[:, fc, dc * 128:(dc + 1) * 128],
                                                    actb[par][:, fc, t0:t0 + n], start=(q_ == 0), stop=(q_ == nmm - 1))
                                    q_ += 1
                            return ins
                        rd = [w[1] for w in wds] + [B_act[w[2]][fc][ti] for w in wds for fc in range(4)]
                        op("pe", fd, reads=rd, writes=[B_pb[bk]])
                        op("dve", lambda v, bk=bk, dc=dc, t0=t0, n=n: v.scalar_tensor_tensor(
                            out=hT[:, dc, t0:t0 + n], in0=banks[bk][:, 0:n], scalar=0.5, in1=hT[:, dc, t0:t0 + n],
                            op0=ALU.mult, op1=ALU.add),
                           reads=[B_pb[bk], B_hT[(dc, ti)]], writes=[B_hT[(dc, ti)]])

            g = 0
            while g < NG:
                grp = [g] if g == NG - 1 else [g, g + 1]
                for gg in grp:
                    gate_up(gg)
                down(grp)
                g += len(grp)
            tr.barrier()

    def win_phase(tiles, own, QT, B_QT, XPt, B_XP, prevX, prevY, B_prev):
        T = T_OWN if own else T_OTH
        loc0 = 1024 if own else 0
        with ExitStack() as st:
            wp_ = WPool("wi", 3, (128, NC_, 256), st)
            kst = [sb("kst%d" % i, (128, 512), BF16, st) for i in range(2)]
            B_kst = [Buf(), Buf()]
            NST = 6
            tst = [sb("tst%d" % i, (128, 256), F32, st) for i in range(NST)]
            B_tst = [Buf() for _ in range(NST)]
            tsb = [sb("tsb%d" % i, (128, 256), BF16, st) for i in range(NST)]
            B_tsb = [Buf() for _ in range(NST)]
            c = {"a": 0, "k": 0, "t": 0}
            units = list(range(16)) if own else list(range(4, 16))
            for u in units:
                wt, wb = wp_.load(wcols(w_in, u * 256, 256))
                kind = u // 4
                if kind in (0, 1, 3):
                    for j in range(2):
                        ch = (u % 4) * 2 + j
                        if kind == 3 and not own:
                            for which, t0 in ((0, 497), (1, 1009)):
                                bk = c["a"] % 2
                                c["a"] += 1

                                def f3(pe, bk=bk, j=j, t0=t0):
                                    ins = None
                                    for k in range(NC_):
                                        ins = pe.matmul(banks[bk][:, 0:15], wt[:, k, j * 128:(j + 1) * 128],
                                                        U[:, k, t0:t0 + 15], start=(k == 0), stop=(k == NC_ - 1))
                                    return ins
                                op("pe", f3, reads=[wb, B_U[which]], writes=[B_pb[bk]])
                                dstp = (prevX if which == 0 else prevY)
                                op("act", lambda a, dstp=dstp, ch=ch, bk=bk: a.copy(dstp[:, ch, :], banks[bk][:, 0:15]),
                                   reads=[B_pb[bk]], writes=[B_prev])
                            continue
                        for ti, (t0, n) in enumerate(tiles):
                            bk = c["a"] % 2
                            c["a"] += 1

                            def ff(pe, bk=bk, j=j, t0=t0, n=n):
                                ins = None
                                for k in range(NC_):
                                    ins = pe.matmul(banks[bk][:, 0:n], wt[:, k, j * 128:(j + 1) * 128],
                                                    U[:, k, t0:t0 + n], start=(k == 0), stop=(k == NC_ - 1))
                                return ins
                            op("pe", ff, reads=[wb, B_U[ti]], writes=[B_pb[bk]])
                            if kind == 0:
                                op("act", lambda a, ch=ch, bk=bk, t0=t0, n=n: a.mul(QT[:, ch, t0:t0 + n], banks[bk][:, 0:n], SB_SCALE),
                                   reads=[B_pb[bk]], writes=[B_QT])
                            elif kind == 1:
                                ks = c["k"] % 2
                                c["k"] += 1
                                op("act", lambda a, ks=ks, bk=bk, n=n: a.copy(kst[ks][:, 0:n], banks[bk][:, 0:n]),
                                   reads=[B_pb[bk]], writes=[B_kst[ks]])
                                dma("sp", kT_scr[ch, :, loc0 + t0:loc0 + t0 + n], kst[ks][:, 0:n],
                                    reads=[B_kst[ks]], writes=[B_kscr])
                            else:
                                op("act", lambda a, ch=ch, bk=bk, ti=ti, n=n: a.copy(XPt[ti][:, ch, 15:15 + n], banks[bk][:, 0:n]),
                                   reads=[B_pb[bk]], writes=[B_XP[ti]])
                if (kind == 1 and own) or kind == 2:
                    col0 = (u % 4) * 256
                    nblk = (T + 127) // 128
                    for b in range(nblk):
                        nb = min(128, T - b * 128)
                        ti = min(b * 128 // 512, 2)
                        bk = 2 + c["t"] % 2
                        c["t"] += 1

                        def ft(pe, bk=bk, b=b, nb=nb):
                            ins = None
                            for k in range(NC_):
                                ins = pe.matmul(banks[bk][0:nb, 0:256], U[:, k, b * 128:b * 128 + nb], wt[:, k, :],
                                                start=(k == 0), stop=(k == NC_ - 1))
                            return ins
                        op("pe", ft, reads=[wb, B_U[ti]], writes=[B_pb[bk]])
                        s = c["t"] % NST
                        if own:
                            op("dve", lambda v, s=s, bk=bk, nb=nb: v.tensor_copy(tst[s][0:nb, :], banks[bk][0:nb, 0:256]),
                               reads=[B_pb[bk]], writes=[B_tst[s]])
                            dstap = (k_own if kind == 1 else v_own)[b * 128:b * 128 + nb, col0:col0 + 256]
                            dma("sp", dstap, tst[s][0:nb, :], reads=[B_tst[s]])
                        if kind == 2:
                            if own:
                                op("act", lambda a, s=s, nb=nb: a.copy(tsb[s][0:nb, :], tst[s][0:nb, :]),
                                   reads=[B_tst[s]], writes=[B_tsb[s]])
                            else:
                                op("act", lambda a, s=s, bk=bk, nb=nb: a.copy(tsb[s][0:nb, :], banks[bk][0:nb, 0:256]),
                                   reads=[B_pb[bk]], writes=[B_tsb[s]])
                            r0 = loc0 + b * 128
                            dma("sp", v_scr[r0:r0 + nb, col0:col0 + 256], tsb[s][0:nb, :],
                                reads=[B_tsb[s]], writes=[B_vscr])
            tr.barrier()

    try:
        stop(1)
        prevX = sb("prevX", (128, 8, 15), F32)
        prevY = sb("prevY", (128, 8, 15), F32)
        B_prev = Buf()

        with ExitStack() as st:
            load_xT(x_oth, T_OTH, st)
            tr.barrier()
        stop(2)
        norm_h_to_U(G_FFN1, TILES_OTH)
        stop(3)
        ffn(w1g, w1u, w1d, TILES_OTH)
        stop(4)
        norm_h_to_U(G_MIX, TILES_OTH)
        win_phase(TILES_OTH, False, None, None, None, None, prevX, prevY, B_prev)
        stop(5)

        with ExitStack() as st:
            load_xT(x_own, T_OWN, st)
            tr.barrier()
        norm_h_to_U(G_FFN1, TILES_FFN)
        ffn(w1g, w1u, w1d, TILES_FFN)
        norm_h_to_U(G_MIX, TILES_OWN)
        stop(7)

        with ExitStack() as mixst:
            QT = sb("QT", (128, 8, T_OWN), BF16, mixst)
            B_QT = Buf()
            xpst = ExitStack()
            XPt = [sb("XP%d" % i, (128, 8, 15 + n), F32, xpst) for i, (t0, n) in enumerate(TILES_OWN)]
            B_XP = [Buf() for _ in range(3)]
            win_phase(TILES_OWN, True, QT, B_QT, XPt, B_XP, prevX, prevY, B_prev)
            stop(8)

            with ExitStack() as st:
                invt = sb("invt", (128, 4, 256), F32, st)
                B_inv = Buf()
                wpl = sb("wpl", (128, 4, 2, 256), BF16, st)
                B_wpl = Buf()
                dma("pool", wpl[:], w_pool.rearrange("g (cc p) d -> p g cc d", p=128), writes=[B_wpl])
                pa = sb("pa", (128, 2, 271), F32, st)
                pbuf = sb("pbuf", (128, 2, 271), F32, st)
                B_pa, B_pbb = Buf(), Buf()
                pa2 = sb("pa2", (128, 2, 271), F32, st)
                pbuf2 = sb("pbuf2", (128, 2, 271), F32, st)
                B_pa2, B_pbb2 = Buf(), Buf()
                pooled = sb("pooled", (128, 8, 256), BF16, st)
                B_pooled = [Buf() for _ in range(4)]
                mxt = sb("mxt", (128, 8, 256), F32, st)
                B_mxt = [Buf() for _ in range(8)]
                B_spt = Buf()
                dma("sp", mxt[0:15, 0:4, :], spool.rearrange("r (a b) -> r a b", a=4), writes=[B_spt])
                op("dve", lambda v: v.tensor_scalar(out=XPt[0][:, :, 0:15], in0=prevY[:], scalar1=flag[:, 0:1], scalar2=None,
                                                    op0=ALU.mult), reads=[B_prev, B_c], writes=[B_XP[0]])
                op("dve", lambda v: v.tensor_copy(XPt[1][:, :, 0:15], prevX[:]), reads=[B_prev], writes=[B_XP[1]])
                for half in range(2):
                    def fsp(pe, half=half):
                        ins = None
                        for j in range(4):
                            cch = half * 4 + j
                            ins = pe.transpose(banks[half][:, j * 128:j * 128 + 15],
                                               mxt[0:15, cch // 2, (cch % 2) * 128:(cch % 2) * 128 + 128],
                                               ident_f[0:15, 0:15])
                        return ins
                    op("pe", fsp, reads=[B_spt, B_c], writes=[B_pb[half]])
                    op("act", lambda a, half=half: a.copy(
                        XPt[2][:, half * 4:half * 4 + 4, 0:15],
                        banks[half][:].rearrange("p (j t) -> p j t", j=4)[:, :, 0:15]),
                       reads=[B_pb[half]], writes=[B_XP[2]])
                tr.barrier()

                subtiles = [(0, 0, 256), (0, 256, 256), (1, 0, 256), (1, 256, 256), (2, 0, 32)]
                for ti, off, n in subtiles:
                    t0 = TILES_OWN[ti][0] + off
                    L = 15 + n
                    X = XPt[ti]
                    dma("sp", invt[:, :, 0:n], invc[:, :, t0:t0 + n], writes=[B_inv])
                    for g in range(4):
                        src_t, src_b = None, B_XP[ti]
                        pe_ = "pool" if g == 3 else "dve"
                        bufs = [(pa2, B_pa2), (pbuf2, B_pbb2)] if g == 3 else [(pa, B_pa), (pbuf, B_pbb)]
                        for s_ in range(g + 1):
                            sh = 1 << s_
                            lo = (1 << (s_ + 1)) - 1
                            dt_, db_ = bufs[s_ % 2]
                            if s_ == 0:
                                a0 = X[:, 2 * g:2 * g + 2, off + lo:off + L]
                                a1 = X[:, 2 * g:2 * g + 2, off + lo - sh:off + L - sh]
                            else:
                                a0 = src_t[:, :, lo:L]
                                a1 = src_t[:, :, lo - sh:L - sh]
                            op(pe_, lambda v, dt_=dt_, a0=a0, a1=a1, lo=lo, L=L: v.tensor_tensor(
                                out=dt_[:, :, lo:L], in0=a0, in1=a1, op=ALU.add), reads=[src_b], writes=[db_])
                            src_t, src_b = dt_, db_
                        ot, ob = bufs[(g + 1) % 2]
                        for cc in range(2):
                            op(pe_, lambda v, cc=cc, ot=ot, g=g, src_t=src_t, L=L, n=n: v.tensor_tensor(
                                out=ot[:, cc, 15:L], in0=src_t[:, cc, 15:L], in1=invt[:, g, 0:n], op=ALU.mult),
                               reads=[src_b, B_inv], writes=[ob])
                        op(pe_, lambda v, ot=ot, g=g, L=L, n=n, X=X, off=off: v.tensor_tensor(
                            out=pooled[:, 2 * g:2 * g + 2, 0:n], in0=ot[:, :, 15:L], in1=X[:, 2 * g:2 * g + 2, off + 15:off + L],
                            op=ALU.subtract), reads=[ob, B_XP[ti]], writes=[B_pooled[g]])
                    for g in range(4):
                        for j in range(2):
                            bk = (g * 2 + j) % 2

                            def fm(pe, g=g, j=j, bk=bk, n=n):
                                ins = None
                                for cc in range(2):
                                    ins = pe.matmul(banks[bk][:, 0:n], wpl[:, g, cc, j * 128:(j + 1) * 128],
                                                    pooled[:, 2 * g + cc, 0:n], start=(cc == 0), stop=(cc == 1))
                                return ins
                            op("pe", fm, reads=[B_wpl, B_pooled[g]], writes=[B_pb[bk]])
                            op("act", lambda a, g=g, j=j, bk=bk, n=n: a.copy(mxt[:, 2 * g + j, 0:n], banks[bk][:, 0:n]),
                               reads=[B_pb[bk]], writes=[B_mxt[2 * g + j]])
                    rmsnorm(mxt, lambda c, _ti: B_mxt[c], 8, G_POOL,
                            lambda c, _t0, n_, t0=t0: U[:, 8 + c, t0:t0 + n_], lambda c, _ti, ti=ti: B_U[ti], [(0, n)], 1024)
                tr.barrier()

                for which, (X, bx, oap, nlast) in enumerate(((XPt[1], B_XP[1], pool_p, 512), (XPt[2], B_XP[2], pool_s, 32))):
                    for half in range(2):
                        def fpo(pe, half=half, X=X, nlast=nlast):
                            ins = None
                            for j in range(4):
                                cch = half * 4 + j
                                ins = pe.transpose(banks[2 + half][0:15, j * 128:(j + 1) * 128], X[:, cch, nlast:nlast + 15],
                                                   ident_f[:, :])
                            return ins
                        op("pe", fpo, reads=[bx, B_c], writes=[B_pb[2 + half]])
                        op("act", lambda a, half=half: a.copy(
                            mxt[0:15, half * 2:half * 2 + 2, :],
                            banks[2 + half][0:15, :].rearrange("p (a b) -> p a b", a=2)),
                           reads=[B_pb[2 + half]], writes=[B_spt])
                    dma("sp", oap.rearrange("r (a b) -> r a b", a=4), mxt[0:15, 0:4, :], reads=[B_spt], writes=[B_spt])
                tr.barrier()
            xpst.close()
            stop(9)

            with ExitStack() as st:
                ATT = sb("ATT", (128, 8, T_OWN), F32, st)
                B_ATT = [Buf() for _ in range(8)]
                dmk = sb("dmk", (128, 4, 512), BF16, st)
                B_dmk = Buf()
                dma("pool", dmk[:], dmask, writes=[B_dmk])
                KT = sb("KT", (128, 2048), BF16, st)
                B_KT = Buf()
                VV = sb("VV", (128, 16, 128), BF16, st)
                B_VV = Buf()
                KTs = sb("KTs", (128, 32), BF16, st)
                VVs = sb("VVs", (32, 128), BF16, st)
                B_KVs = Buf()
                CKr = sb("CKr", (128, 16, 128), BF16, st)
                B_CKr = Buf()
                CKT = sb("CKT", (128, 2048), BF16, st)
                B_CKT = Buf()
                CV = sb("CV", (128, 16, 128), BF16, st)
                B_CV = Buf()
                widths = [32, 512, 512]
                e2 = [[sb("e2_%d_%d" % (s_, i), (128, widths[s_]), F32, st) for i in range(2)] for s_ in range(3)]
                lp = [[sb("lp_%d_%d" % (s_, i), (128, widths[s_]), BF16, st) for i in range(2)] for s_ in range(3)]
                ab = [[sb("ab_%d_%d" % (s_, i), (128, widths[s_]), BF16, st) for i in range(2)] for s_ in range(3)]
                Rb = [sb("R%d" % s_, (128, widths[s_]), BF16, st) for s_ in range(3)]
                B_e2 = [[Buf(), Buf()] for _ in range(3)]
                B_lp = [[Buf(), Buf()] for _ in range(3)]
                B_ab = [[Buf(), Buf()] for _ in range(3)]
                B_R = [Buf() for _ in range(3)]
                ctr = {"z": 0, "s": 0}

                def nxt(key, n):
                    v = ctr[key] % n
                    ctr[key] += 1
                    return v

                def load_prompt(h):
                    dma("sp", KT[:], kT_scr[h, :, 0:2048], reads=[B_kscr], writes=[B_KT])
                    dma("sp", VV[:], v_scr[0:2048, h * 128:(h + 1) * 128].rearrange("(b p) d -> p b d", p=128),
                        reads=[B_vscr], writes=[B_VV])

                def load_sample(h):
                    dma("sp", KTs[:], kT_scr[h, :, 2048:2080], reads=[B_kscr], writes=[B_KVs])
                    dma("sp", VVs[:], v_scr[2048:2080, h * 128:(h + 1) * 128], reads=[B_vscr], writes=[B_KVs])
                    dma("pool", CKr[:], ck[:, h * 128:(h + 1) * 128].rearrange("(b p) d -> p b d", p=128), writes=[B_CKr])
                    dma("pool", CV[:], cv[:, h * 128:(h + 1) * 128].rearrange("(b p) d -> p b d", p=128), writes=[B_CV])
                    for half in range(2):
                        def ftk(pe, half=half):
                            ins = None
                            for j in range(8):
                                b_ = half * 8 + j
                                ins = pe.transpose(pb6_bf[:, j * 128:(j + 1) * 128], CKr[:, b_, :], ident_b)
                            return ins
                        op("pe", ftk, reads=[B_CKr, B_c], writes=[B_pb[6]])
                        op("dve", lambda v, half=half: v.tensor_copy(CKT[:, half * 1024:(half + 1) * 1024], pb6_bf[:, :]),
                           reads=[B_pb[6]], writes=[B_CKT])

                def job_gen(h, q0, nq, steps, ob, sid):
                    ns = len(steps)
                    qap = QT[:, h, q0:q0 + nq]
                    R, bR = Rb[sid], B_R[sid]

                    def mask_mul(buf, bbuf, nk, mask):
                        if mask is None:
                            return
                        if mask[0] == "diag":
                            mi = mask[1]
                            op("dve", lambda v: v.tensor_tensor(out=buf[0:nk, 0:nq], in0=buf[0:nk, 0:nq],
                                                                in1=dmk[0:nk, mi, 0:nq], op=ALU.mult),
                               reads=[bbuf, B_dmk], writes=[bbuf])
                        else:
                            op("dve", lambda v: v.tensor_scalar(out=buf[0:nk, 0:nq], in0=buf[0:nk, 0:nq],
                                                                scalar1=flag[0:nk, 0:1], scalar2=None, op0=ALU.mult),
                               reads=[bbuf, B_c], writes=[bbuf])

                    def st_z(i):
                        kap, kbuf, vap, vbuf, nk, mask = steps[i]
                        zb = nxt("z", 2)
                        op("pe", lambda pe: pe.matmul(banks[zb][0:nk, 0:nq], kap, qap, start=True, stop=True),
                           reads=[kbuf, B_QT], writes=[B_pb[zb]])
                        return zb

                    def st_e(i, zb):
                        kap, kbuf, vap, vbuf, nk, mask = steps[i]
                        e, be = e2[sid][i % 2], B_e2[sid][i % 2]
                        op("act", lambda a: a.activation(e[0:nk, 0:nq], banks[zb][0:nk, 0:nq], AF.Exp),
                           reads=[B_pb[zb]], writes=[be])
                        mask_mul(e, be, nk, mask)

                    def st_l(i):
                        kap, kbuf, vap, vbuf, nk, mask = steps[i]
                        op("act", lambda a: a.activation(lp[sid][i % 2][0:nk, 0:nq], e2[sid][i % 2][0:nk, 0:nq], AF.Ln,
                                                         bias=1.0, scale=1.0),
                           reads=[B_e2[sid][i % 2]], writes=[B_lp[sid][i % 2]])

                    def st_s(i):
                        kap, kbuf, vap, vbuf, nk, mask = steps[i]
                        l, bl = lp[sid][i % 2], B_lp[sid][i % 2]
                        a_, ba = ab[sid][i % 2], B_ab[sid][i % 2]
                        sbk = 2 + nxt("s", 2)

                        def fs(pe):
                            pe.matmul(banks[sbk][0:nk, 0:nq], ntri_b[0:nk, 0:nk], l[0:nk, 0:nq], start=True, stop=False)
                            if i > 0:
                                pe.matmul(banks[sbk][0:nk, 0:nq], nones_b[:, 0:nk], R[:, 0:nq], start=False, stop=False)
                            return pe.matmul(banks[sbk][0:nk, 0:nq], kap, qap, start=False, stop=True)
                        op("pe", fs, reads=[bl, B_c, kbuf, B_QT] + ([bR] if i > 0 else []), writes=[B_pb[sbk]])
                        if i < ns - 1:
                            if i == 0:
                                if nk < 128:
                                    op("pool", lambda g: g.memset(R[:, 0:nq], 0.0), writes=[bR])
                                op("pool", lambda g: g.tensor_copy(R[0:nk, 0:nq], l[0:nk, 0:nq]), reads=[bl], writes=[bR])
                            else:
                                op("pool", lambda g: g.tensor_tensor(out=R[0:nk, 0:nq], in0=R[0:nk, 0:nq], in1=l[0:nk, 0:nq],
                                                                     op=ALU.add), reads=[bl, bR], writes=[bR])
                        op("act", lambda a: a.activation(a_[0:nk, 0:nq], banks[sbk][0:nk, 0:nq], AF.Exp),
                           reads=[B_pb[sbk]], writes=[ba])
                        mask_mul(a_, ba, nk, mask)

                    def st_v(i):
                        kap, kbuf, vap, vbuf, nk, mask = steps[i]
                        op("pe", lambda pe: pe.matmul(banks[ob][:, 0:nq], vap, ab[sid][i % 2][0:nk, 0:nq],
                                                      start=(i == 0), stop=(i == ns - 1)),
                           reads=[vbuf, B_ab[sid][i % 2]], writes=[B_pb[ob]])

                    for t in range(ns + 3):
                        zb = st_z(t) if t < ns else None
                        if 0 <= t - 1 < ns:
                            st_l(t - 1)
                        if t < ns:
                            st_e(t, zb)
                        if 0 <= t - 2 < ns:
                            st_s(t - 2)
                        if 0 <= t - 3 < ns:
                            st_v(t - 3)
                        yield
                    op("act", lambda a: a.copy(ATT[:, h, q0:q0 + nq], banks[ob][:, 0:nq]), reads=[B_pb[ob]], writes=[B_ATT[h]])

                def tile_steps(loc, mask_kind):
                    out = []
                    for kb in (3, 2, 1, 0):
                        t0 = loc + kb * 128
                        if mask_kind == "diag":
                            m = ("diag", kb)
                        elif mask_kind == "flag":
                            m = ("flag",)
                        else:
                            m = None
                        out.append((KT[:, t0:t0 + 128], B_KT, VV[:, t0 // 128, :], B_VV, 128, m))
                    return out

                LOX, LOY, LS0, LS1 = 0, 512, 1024, 1536
                for h in range(8):
                    load_sample(h)
                    load_prompt(h)
                    st_s = [(KTs[:, :], B_KVs, VVs[:, :], B_KVs, 32, ("diag", 0))]
                    for b_ in range(15, -1, -1):
                        st_s.append((CKT[:, b_ * 128:(b_ + 1) * 128], B_CKT, CV[:, b_, :], B_CV, 128, None))
                    gens = [
                        job_gen(h, 1024, 32, st_s, 4, 0),
                        job_gen(h, 512, 512, tile_steps(LS1, "diag") + tile_steps(LOX, None) + tile_steps(LS0, None)
                                + tile_steps(LOY, "flag"), 5, 1),
                        job_gen(h, 0, 512, tile_steps(LS0, "diag") + tile_steps(LOY, "flag"), 7, 2),
                    ]
                    while gens:
                        for g_ in list(gens):
                            try:
                                next(g_)
                            except StopIteration:
                                gens.remove(g_)

                rmsnorm(ATT, lambda c, ti: B_ATT[c], 8, G_ATT, lambda c, t0, n: U[:, c, t0:t0 + n],
                        lambda c, ti: B_U[ti], TILES_OWN, 1024)
                tr.barrier()

            stop(10)
            with ExitStack() as st:
                wo = WPool("wo", 3, (128, NC_, 256), st)
                cn = 0
                for u in range(8):
                    wt, wb = wo.load(wcols(w_out, u * 256, 256))
                    for ti, (t0, n) in enumerate(TILES_OWN):
                        for j in range(2):
                            dc = u * 2 + j
                            bk = cn % 2
                            cn += 1

                            def fo(pe, bk=bk, j=j, t0=t0, n=n, wt=wt):
                                ins = None
                                for k in range(NC_):
                                    ins = pe.matmul(banks[bk][:, 0:n], wt[:, k, j * 128:(j + 1) * 128], U[:, k, t0:t0 + n],
                                                    start=(k == 0), stop=(k == NC_ - 1))
                                return ins
                            op("pe", fo, reads=[wb, B_U[ti]], writes=[B_pb[bk]])
                            op("dve", lambda v, bk=bk, dc=dc, t0=t0, n=n: v.tensor_tensor(
                                out=hT[:, dc, t0:t0 + n], in0=hT[:, dc, t0:t0 + n], in1=banks[bk][:, 0:n], op=ALU.add),
                               reads=[B_pb[bk], B_hT[(dc, ti)]], writes=[B_hT[(dc, ti)]])
                tr.barrier()
        tr.barrier()

        stop(11)
        norm_h_to_U(G_FFN2, TILES_FFN)
        ffn(w2g, w2u, w2d, TILES_FFN)

        stop(12)
        norm_h_to_U(G_PLE, TILES_FFN)
        with ExitStack() as st:
            PT = sb("PT", (128, 2, T_OWN), BF16, st)
            B_PT = Buf()
            pin = [sb("pin%d" % i, (128, PLE), F32, st) for i in range(2)]
            B_pin = [Buf(), Buf()]
            wppt = sb("wppt", (128, 2, D), BF16, st)
            B_wpp = Buf()
            dma("pool", wppt[:], wpp.rearrange("(c p) f -> p c f", p=128), writes=[B_wpp])
            nblk = (T_OWN + 127) // 128
            for b in range(nblk):
                nb = min(128, T_OWN - b * 128)
                i = b % 2
                dma("sp", pin[i][0:nb, :], p_own[b * 128:b * 128 + nb, :], writes=[B_pin[i]])

                def fp(pe, i=i, nb=nb):
                    ins = None
                    for j in range(2):
                        ins = pe.transpose(banks[i][:, j * 128:j * 128 + nb], pin[i][0:nb, j * 128:(j + 1) * 128], ident_f[0:nb, 0:nb])
                    return ins
                op("pe", fp, reads=[B_pin[i], B_c], writes=[B_pb[i]])
                op("act", lambda a, i=i, b=b, nb=nb: a.copy(
                    PT[:, :, b * 128:b * 128 + nb], banks[i][:, 0:256].rearrange("p (j t) -> p j t", j=2)[:, :, 0:nb]),
                   reads=[B_pb[i]], writes=[B_PT])
            wg_ = WPool("wpg", 3, (128, NC_, 256), st)
            sig = [sb("sig%d" % i, (128, 512), F32, st) for i in range(2)]
            B_sig = [Buf(), Buf()]
            cn = 0
            for u in range(8):
                wt, wb = wg_.load(wcols(wpg, u * 256, 256))
                for ti, (t0, n) in enumerate(TILES_FFN):
                    for j in range(2):
                        dc = u * 2 + j
                        gb = cn % 2
                        pbk = 2 + cn % 2
                        cn += 1

                        def fgate(pe, gb=gb, j=j, t0=t0, n=n, wt=wt):
                            ins = None
                            for k in range(NC_):
                                ins = pe.matmul(banks[gb][:, 0:n], wt[:, k, j * 128:(j + 1) * 128], U[:, k, t0:t0 + n],
                                                start=(k == 0), stop=(k == NC_ - 1))
                            return ins
                        op("pe", fgate, reads=[wb, B_U[ti]], writes=[B_pb[gb]])

                        def fproj(pe, pbk=pbk, dc=dc, t0=t0, n=n):
                            ins = None
                            for k in range(2):
                                ins = pe.matmul(banks[pbk][:, 0:n], wppt[:, k, dc * 128:(dc + 1) * 128], PT[:, k, t0:t0 + n],
                                                start=(k == 0), stop=(k == 1))
                            return ins
                        op("pe", fproj, reads=[B_wpp, B_PT], writes=[B_pb[pbk]])
                        s = cn % 2
                        op("act", lambda a, s=s, gb=gb, n=n: a.activation(sig[s][:, 0:n], banks[gb][:, 0:n], AF.Sigmoid),
                           reads=[B_pb[gb]], writes=[B_sig[s]])
                        op("dve", lambda v, s=s, pbk=pbk, n=n: v.tensor_tensor(out=sig[s][:, 0:n], in0=sig[s][:, 0:n],
                                                                              in1=banks[pbk][:, 0:n], op=ALU.mult),
                           reads=[B_sig[s], B_pb[pbk]], writes=[B_sig[s]])
                        op("dve", lambda v, s=s, dc=dc, t0=t0, n=n: v.tensor_tensor(out=hT[:, dc, t0:t0 + n], in0=hT[:, dc, t0:t0 + n],
                                                                                    in1=sig[s][:, 0:n], op=ALU.add),
                           reads=[B_sig[s], B_hT[(dc, ti)]], writes=[B_hT[(dc, ti)]])
            tr.barrier()

        stop(13)
        rmsnorm(hT, hbuf, NC_, G_FIN, lambda c, t0, n: hT[:, c, t0:t0 + n], hbuf, TILES_OWN, D)
        with ExitStack() as st:
            yo = [sb("yo%d" % i, (128, D), F32, st) for i in range(3)]
            B_yo = [Buf(), Buf(), Buf()]
            nblk = (T_OWN + 127) // 128
            q = 0
            for b in range(nblk):
                nb = min(128, T_OWN - b * 128)
                ti = min(b * 128 // 512, 2)
                i = b % 3
                for cg in range(4):
                    bk = q % 2
                    q += 1

                    def fy(pe, bk=bk, cg=cg, b=b, nb=nb):
                        ins = None
                        for j in range(4):
                            c = cg * 4 + j
                            ins = pe.transpose(banks[bk][0:nb, j * 128:(j + 1) * 128], hT[:, c, b * 128:b * 128 + nb], ident_f[:, :])
                        return ins
                    op("pe", fy, reads=[B_hT[(cg * 4 + j, ti)] for j in range(4)] + [B_c], writes=[B_pb[bk]])
                    if q % 2:
                        op("act", lambda a, i=i, bk=bk, cg=cg, nb=nb: a.copy(yo[i][0:nb, cg * 512:(cg + 1) * 512], banks[bk][0:nb, :]),
                           reads=[B_pb[bk]], writes=[B_yo[i]])
                    else:
                        op("dve", lambda v, i=i, bk=bk, cg=cg, nb=nb: v.tensor_copy(yo[i][0:nb, cg * 512:(cg + 1) * 512], banks[bk][0:nb, :]),
                           reads=[B_pb[bk]], writes=[B_yo[i]])
                dma("sp", y_own[b * 128:b * 128 + nb, :], yo[i][0:nb, :], reads=[B_yo[i]])
            tr.finish()

    except _Stop:
        tr.finish()
    return nc


_CACHE = {}


def _consts():
    cm = np.zeros((128, 4, 128), np.float32)
    cm[:, 0, :] = np.eye(128, dtype=np.float32)
    cm[:, 1, :] = 1.0
    j = np.arange(128)[:, None]
    k = np.arange(128)[None, :]
    cm[:, 2, :] = np.where(j >= k, -1.0, 0.0)
    cm[:, 3, :] = -1.0
    dm = np.zeros((128, 4, 512), np.float32)
    q = np.arange(512)[None, :]
    for i in range(4):
        dm[:, i, :] = ((128 * i + np.arange(128)[:, None]) < q).astype(np.float32)
    return cm, dm


def kernel(x_prompt, x_sample, cache_k, cache_v, state_pool, p_prompt, p_sample,
           g_ffn1, w1_gate, w1_up, w1_down, g_mix, w_in, g_attn_out, w_pool, pool_scale,
           w_out, g_ffn2, w2_gate, w2_up, w2_down, g_ple, w_ple_gate, w_ple_proj, g_final):
    f = lambda a: np.ascontiguousarray(np.asarray(a, dtype=np.float32))
    x_prompt, x_sample, cache_k, cache_v, state_pool, p_prompt, p_sample = map(
        f, (x_prompt, x_sample, cache_k, cache_v, state_pool, p_prompt, p_sample))
    if "nc" not in _CACHE:
        _CACHE["nc"] = build_program()
    nc = _CACHE["nc"]

    def g16(g):
        return f(g).reshape(-1, 128).T
    gains = np.ascontiguousarray(np.concatenate(
        [g16(g_ffn1[0]), g16(g_mix[0]), g16(g_ffn2[0]), g16(g_ple[0]), g16(g_final), g16(g_attn_out[0]),
         g16(pool_scale[0])], axis=1))
    cm, dm = _consts()
    shared = {
        "w1g": f(w1_gate[0]), "w1u": f(w1_up[0]), "w1d": f(w1_down[0]),
        "w2g": f(w2_gate[0]), "w2u": f(w2_up[0]), "w2d": f(w2_down[0]),
        "w_in": f(w_in[0]), "w_pool": f(w_pool[0]), "w_out": f(w_out[0]),
        "wpg": f(w_ple_gate[0]), "wpp": f(w_ple_proj[0]), "gains": gains, "cmat": cm, "dmask": dm,
    }
    wins = np.array([2, 4, 8, 16], np.float32)
    own_tiles = {0: (0, 2), 1: (1, 3)}
    oth_tiles = {0: (1, 3), 1: (2, 0)}
    in_maps = []
    for c in range(8):
        p, r = c // 2, c % 2
        s0, s1 = own_tiles[r]
        ox, oy = oth_tiles[r]
        tl = lambda a, g: a[p, g * 512:(g + 1) * 512]
        pos = np.concatenate([np.arange(s0 * 512, s0 * 512 + 512), np.arange(s1 * 512, s1 * 512 + 512),
                              2048 + np.arange(32)]).astype(np.float32)
        cnt = np.minimum(pos[None, :] + 1.0, wins[:, None])
        invc = np.ascontiguousarray(np.broadcast_to((1.0 / cnt)[None], (128, 4, T_OWN))).astype(np.float32)
        m = dict(shared)
        m.update({
            "x_own": np.ascontiguousarray(np.concatenate([tl(x_prompt, s0), tl(x_prompt, s1), x_sample[c]], axis=0)),
            "x_oth": np.ascontiguousarray(np.concatenate([tl(x_prompt, ox), tl(x_prompt, oy)], axis=0)),
            "p_own": np.ascontiguousarray(np.concatenate([tl(p_prompt[0], s0), tl(p_prompt[0], s1), p_sample[0, c]], axis=0)),
            "ck": np.ascontiguousarray(cache_k[0, c].reshape(2048, 1024)),
            "cv": np.ascontiguousarray(cache_v[0, c].reshape(2048, 1024)),
            "spool": np.ascontiguousarray(state_pool[0, c]),
            "flagv": np.full((128, 1), float(r), np.float32),
            "invc": invc,
        })
        in_maps.append(m)
    res = run_bass_kernel_spmd(nc, in_maps, core_ids=list(range(8)))
    R = res.results
    y_prompt = np.zeros((4, 2048, D), np.float32)
    y_sample = np.zeros((8, 32, D), np.float32)
    nkp = np.zeros((1, 4, 2048, 8, 128), np.float32)
    nvp = np.zeros((1, 4, 2048, 8, 128), np.float32)
    npp = np.zeros((1, 4, 15, 1024), np.float32)
    nks = np.zeros((1, 8, 32, 8, 128), np.float32)
    nvs = np.zeros((1, 8, 32, 8, 128), np.float32)
    nps = np.zeros((1, 8, 15, 1024), np.float32)
    for c in range(8):
        p, r = c // 2, c % 2
        o = R[c]
        for slot, g in enumerate(own_tiles[r]):
            sl = slice(g * 512, (g + 1) * 512)
            y_prompt[p, sl] = o["y_own"][slot * 512:(slot + 1) * 512]
            nkp[0, p, sl] = o["k_own"][slot * 512:(slot + 1) * 512].reshape(512, 8, 128)
            nvp[0, p, sl] = o["v_own"][slot * 512:(slot + 1) * 512].reshape(512, 8, 128)
        y_sample[c] = o["y_own"][1024:1056]
        nks[0, c] = o["k_own"][1024:1056].reshape(32, 8, 128)
        nvs[0, c] = o["v_own"][1024:1056].reshape(32, 8, 128)
        nps[0, c] = o["pool_s"]
        if r == 1:
            npp[0, p] = o["pool_p"]
    return (y_prompt, y_sample, nkp, nvp, npp, nks, nvs, nps)
```

```python
import numpy as np
from contextlib import ExitStack
import concourse.bass as bass
import concourse.mybir as mybir
from concourse.bass_utils import run_bass_kernel_spmd

F32 = mybir.dt.float32
BF16 = mybir.dt.bfloat16
AF = mybir.ActivationFunctionType
ALU = mybir.AluOpType

D = 2048
NC_ = 16
DFF = 5632
NG = 11
T_OWN = 1056
T_OTH = 1024
TLOC = 2080
PLE = 256
EPS = 1e-6
SB_SCALE = 128.0 ** -0.5
TILES_OWN = [(0, 512), (512, 512), (1024, 32)]
TILES_OTH = [(0, 512), (512, 512)]
TILES_FFN = [(0, 352), (352, 352), (704, 352)]
NDS = 28


class Tok:
    __slots__ = ("si", "val", "eng")

    def __init__(self, si, val, eng):
        self.si = si
        self.val = val
        self.eng = eng


class Buf:
    __slots__ = ("w", "r")

    def __init__(self):
        self.w = None
        self.r = {}


class TR:
    def __init__(self, nc, es):
        self.nc = nc
        self.sems = []
        self.eng = {}
        for name, h in (("pe", nc.tensor), ("act", nc.scalar), ("dve", nc.vector),
                        ("pool", nc.gpsimd), ("sp", nc.sync)):
            s = es.enter_context(nc.semaphore("sem_" + name))
            self.sems.append(s)
            self.eng[name] = dict(h=h, si=len(self.sems) - 1, cnt=0, waited={})
        self.dsq = {"sp": [], "pool": []}
        self.ds = []
        for q, n in (("sp", NDS), ("pool", 12)):
            for i in range(n):
                s = es.enter_context(nc.semaphore("dsem_%s%d" % (q, i)))
                self.sems.append(s)
                d = dict(si=len(self.sems) - 1, cnt=0)
                self.dsq[q].append(d)
                self.ds.append(d)
        self.rr = {"sp": 0, "pool": 0}

    def wait(self, en, tok):
        if tok is None:
            return
        if tok.eng == en and en == "pe":
            return
        e = self.eng[en]
        if e["waited"].get(tok.si, 0) >= tok.val:
            return
        e["h"].wait_ge(self.sems[tok.si], tok.val)
        e["waited"][tok.si] = tok.val

    def _deps(self, en, reads, writes):
        for b in reads:
            for t in (b.w if isinstance(b.w, list) else [b.w]):
                self.wait(en, t)
        for b in writes:
            for t in (b.w if isinstance(b.w, list) else [b.w]):
                self.wait(en, t)
            for t in b.r.values():
                self.wait(en, t)

    def _commit(self, tok, reads, writes):
        for b in reads:
            b.r[tok.si] = tok
        for b in writes:
            b.w = tok
            b.r = {}

    def op(self, en, fn, reads=(), writes=()):
        self._deps(en, reads, writes)
        e = self.eng[en]
        ins = fn(e["h"])
        e["cnt"] += 1
        ins.then_inc(self.sems[e["si"]], 1)
        tok = Tok(e["si"], e["cnt"], en)
        self._commit(tok, reads, writes)
        return tok

    def dma(self, en, out, in_, reads=(), writes=()):
        d = self.dsq[en][self.rr[en]]
        self.rr[en] = (self.rr[en] + 1) % len(self.dsq[en])
        if d["cnt"] > 0:
            self.wait(en, Tok(d["si"], d["cnt"], None))
        self._deps(en, reads, writes)
        ins = self.eng[en]["h"].dma_start(out=out, in_=in_)
        d["cnt"] += 16
        ins.then_inc(self.sems[d["si"]], 16)
        tok = Tok(d["si"], d["cnt"], None)
        self._commit(tok, reads, writes)
        return tok

    def barrier(self):
        for en in self.eng:
            for fn_, f in self.eng.items():
                if f["cnt"] > 0 and not (fn_ == en and en in ("pe", "sp")):
                    self.wait(en, Tok(f["si"], f["cnt"], fn_))
            for d in self.ds:
                if d["cnt"] > 0:
                    self.wait(en, Tok(d["si"], d["cnt"], None))

    def finish(self):
        self.barrier()


class _Stop(Exception):
    pass


def build_program(stop_at=None):
    nc = bass.Bass("TRN2", target_bir_lowering=False)
    es = ExitStack()
    E = es.enter_context

    def din(name, shape):
        return nc.dram_tensor(name, list(shape), F32, kind="ExternalInput").ap()

    def dout(name, shape):
        return nc.dram_tensor(name, list(shape), F32, kind="ExternalOutput").ap()

    x_own = din("x_own", (T_OWN, D))
    x_oth = din("x_oth", (T_OTH, D))
    p_own = din("p_own", (T_OWN, PLE))
    ck = din("ck", (2048, 1024))
    cv = din("cv", (2048, 1024))
    spool = din("spool", (15, 1024))
    w1g = din("w1g", (D, DFF)); w1u = din("w1u", (D, DFF)); w1d = din("w1d", (DFF, D))
    w2g = din("w2g", (D, DFF)); w2u = din("w2u", (D, DFF)); w2d = din("w2d", (DFF, D))
    w_in = din("w_in", (D, 4096))
    w_pool = din("w_pool", (4, 256, 256))
    w_out = din("w_out", (D, D))
    wpg = din("wpg", (D, D))
    wpp = din("wpp", (PLE, D))
    gains = din("gains", (128, 96))
    cmat = din("cmat", (128, 4, 128))
    dmask = din("dmask", (128, 4, 512))
    flagv = din("flagv", (128, 1))
    invc = din("invc", (128, 4, T_OWN))

    y_own = dout("y_own", (T_OWN, D))
    k_own = dout("k_own", (T_OWN, 1024))
    v_own = dout("v_own", (T_OWN, 1024))
    pool_p = dout("pool_p", (15, 1024))
    pool_s = dout("pool_s", (15, 1024))

    kT_scr = nc.dram_tensor("kT_scr", [8, 128, TLOC], BF16, kind="Internal").ap()
    v_scr = nc.dram_tensor("v_scr", [TLOC, 1024], BF16, kind="Internal").ap()

    tr = TR(nc, es)
    op, dma = tr.op, tr.dma

    def stop(n):
        if stop_at is not None and stop_at == n:
            raise _Stop()

    uid = [0]

    def sb(name, shape, dt, stack=None):
        uid[0] += 1
        return (stack or es).enter_context(nc.sbuf_tensor("%s_%d" % (name, uid[0]), list(shape), dt))

    hT = sb("hT", (128, NC_, T_OWN), F32)
    U = sb("U", (128, NC_, T_OWN), BF16)
    cm_f = sb("cm_f", (128, 4, 128), F32)
    cm_b = sb("cm_b", (128, 4, 128), BF16)
    gn = sb("gn", (128, 96), F32)
    flag = sb("flag", (128, 1), F32)
    rstd = sb("rstd", (128, 512), F32)
    sqt = [sb("sqt%d" % i, (128, 512), BF16) for i in range(3)]
    B_hT = {(c, i): Buf() for c in range(NC_) for i in range(3)}
    B_U = [Buf() for _ in range(3)]
    B_c = Buf()
    B_rstd = Buf()
    B_sq = [Buf() for _ in range(3)]
    B_kscr = Buf()
    B_vscr = Buf()

    banks = [E(nc.psum_tensor("pb%d" % i, [128, 512], F32)) for i in range(8)]
    B_pb = [Buf() for _ in range(8)]
    pb6_bf = banks[6].bitcast(BF16)

    ident_f = cm_f[:, 0, :]
    ident_b = cm_b[:, 0, :]
    ones_b = cm_b[:, 1, :]
    ntri_b = cm_b[:, 2, :]
    nones_b = cm_b[:, 3, :]

    dma("sp", cm_f[:], cmat, writes=[B_c])
    dma("sp", gn[:], gains, writes=[B_c])
    dma("sp", flag[:], flagv, writes=[B_c])
    dma("pool", cm_b[:], cmat, writes=[B_c])

    G_FFN1, G_MIX, G_FFN2, G_PLE, G_FIN, G_ATT, G_POOL = 0, 16, 32, 48, 64, 80, 88

    def load_xT(x_ap, T, stack):
        xin = [sb("xin%d" % i, (128, D), F32, stack) for i in range(3)]
        B_xin = [Buf(), Buf(), Buf()]
        nblk = (T + 127) // 128
        q = 0
        for b in range(nblk):
            nb = min(128, T - b * 128)
            xi, bx = xin[b % 3], B_xin[b % 3]
            dma("sp", xi[0:nb, :], x_ap[b * 128:b * 128 + nb, :], writes=[bx])
            for cg in range(4):
                pbi = q % 2
                q += 1

                def f(pe, cg=cg, nb=nb, xi=xi, pbi=pbi):
                    ins = None
                    for j in range(4):
                        c = cg * 4 + j
                        ins = pe.transpose(banks[pbi][:, j * 128:j * 128 + nb], xi[0:nb, c * 128:(c + 1) * 128],
                                           ident_f[0:nb, 0:nb])
                    return ins
                op("pe", f, reads=[bx, B_c], writes=[B_pb[pbi]])
                ti = min(b * 128 // 512, 2)
                src = banks[pbi][:].rearrange("p (j t) -> p j t", j=4)[:, :, 0:nb]
                dst = hT[:, cg * 4:cg * 4 + 4, b * 128:b * 128 + nb]
                eng = "act" if (q % 2) else "dve"
                if eng == "act":
                    op("act", lambda a, dst=dst, src=src: a.copy(dst, src), reads=[B_pb[pbi]],
                       writes=[B_hT[(cg * 4 + j, ti)] for j in range(4)])
                else:
                    op("dve", lambda v, dst=dst, src=src: v.tensor_copy(dst, src), reads=[B_pb[pbi]],
                       writes=[B_hT[(cg * 4 + j, ti)] for j in range(4)])

    def rmsnorm(src, srcbufs, C, goff, dst, dstbufs, tiles, width):
        for ti, (t0, n) in enumerate(tiles):
            for c in range(C):
                k = c % 3
                op("act", lambda a, c=c, k=k: a.activation(sqt[k][:, 0:n], src[:, c, t0:t0 + n], AF.Square),
                   reads=[srcbufs(c, ti)], writes=[B_sq[k]])
                op("pe", lambda pe, c=c, k=k: pe.matmul(banks[7][:, 0:n], ones_b, sqt[k][:, 0:n],
                                                        start=(c == 0), stop=(c == C - 1)),
                   reads=[B_sq[k], B_c], writes=[B_pb[7]])
            op("act", lambda a: a.activation(rstd[:, 0:n], banks[7][:, 0:n], AF.Sqrt, bias=EPS, scale=1.0 / width),
               reads=[B_pb[7]], writes=[B_rstd])
            op("dve", lambda v: v.reciprocal(rstd[:, 0:n], rstd[:, 0:n]), reads=[B_rstd], writes=[B_rstd])
            multi = {}
            for c in range(C):
                d = dstbufs(c, ti)
                tmp = Buf()
                tmp.w = d.w
                tmp.r = dict(d.r)
                tok = op("dve", lambda v, c=c: v.scalar_tensor_tensor(
                    out=dst(c, t0, n), in0=src[:, c, t0:t0 + n], scalar=gn[:, goff + c:goff + c + 1],
                    in1=rstd[:, 0:n], op0=ALU.mult, op1=ALU.mult),
                    reads=[srcbufs(c, ti), B_rstd, B_c], writes=[tmp])
                multi.setdefault(id(d), (d, []))[1].append(tok)
            for d, toks in multi.values():
                d.w = toks
                d.r = {}

    def hbuf(c, ti):
        return B_hT[(c, ti)]

    def norm_h_to_U(goff, tiles):
        rmsnorm(hT, hbuf, NC_, goff, lambda c, t0, n: U[:, c, t0:t0 + n], lambda c, ti: B_U[ti], tiles, D)

    class WPool:
        def __init__(self, name, n, shape, stack):
            self.t = [sb("%s%d" % (name, i), shape, BF16, stack) for i in range(n)]
            self.b = [Buf() for _ in range(n)]
            self.i = 0

        def load(self, src_ap):
            i = self.i
            self.i = (self.i + 1) % len(self.t)
            dma("pool", self.t[i][:], src_ap, writes=[self.b[i]])
            return self.t[i], self.b[i]

    def wcols(w_ap, c0, ncol):
        return w_ap.rearrange("(c p) f -> p c f", p=128)[:, :, c0:c0 + ncol]

    def ffn(wg, wu, wd, tiles):
        with ExitStack() as st:
            gu = WPool("gu", 4, (128, NC_, 256), st)
            wdp = WPool("wd", 2, (128, 4, D), st)
            actb = [sb("act%d" % i, (128, 4, T_OWN), BF16, st) for i in range(2)]
            B_act = [[[Buf() for _ in range(3)] for _ in range(4)] for _ in range(2)]
            sg = [sb("sg%d" % i, (128, 512), F32, st) for i in range(2)]
            B_sg = [Buf(), Buf()]
            cnt = {"gu": 0, "dn": 0, "sg": 0}

            def gate_up(g):
                par = g % 2
                for pr in range(2):
                    wgt, wgb = gu.load(wcols(wg, g * 512 + pr * 256, 256))
                    wut, wub = gu.load(wcols(wu, g * 512 + pr * 256, 256))
                    for ti, (t0, n) in enumerate(tiles):
                        for j in range(2):
                            fc = pr * 2 + j
                            gb = cnt["gu"] % 2
                            ub = 2 + cnt["gu"] % 2
                            cnt["gu"] += 1

                            def fg(pe, wt=wgt, bk=gb, j=j, t0=t0, n=n):
                                ins = None
                                for k in range(NC_):
                                    ins = pe.matmul(banks[bk][:, 0:n], wt[:, k, j * 128:(j + 1) * 128],
                                                    U[:, k, t0:t0 + n], start=(k == 0), stop=(k == NC_ - 1))
                                return ins
                            op("pe", fg, reads=[wgb, B_U[ti]], writes=[B_pb[gb]])
                            op("pe", lambda pe, wt=wut, bk=ub, j=j, t0=t0, n=n: fg(pe, wt, bk, j, t0, n),
                               reads=[wub, B_U[ti]], writes=[B_pb[ub]])
                            si = cnt["sg"] % 2
                            cnt["sg"] += 1
                            op("act", lambda a, si=si, gb=gb, n=n: a.activation(sg[si][:, 0:n], banks[gb][:, 0:n], AF.Silu),
                               reads=[B_pb[gb]], writes=[B_sg[si]])
                            op("dve", lambda v, si=si, ub=ub, fc=fc, t0=t0, n=n, par=par: v.tensor_tensor(
                                out=actb[par][:, fc, t0:t0 + n], in0=sg[si][:, 0:n], in1=banks[ub][:, 0:n], op=ALU.mult),
                               reads=[B_sg[si], B_pb[ub]], writes=[B_act[par][fc][ti]])

            def down(grp):
                wds = []
                for g in grp:
                    wds.append(wdp.load(wd[g * 512:(g + 1) * 512, :].rearrange("(c p) f -> p c f", p=128)) + (g % 2,))
                nmm = 4 * len(grp)
                for ti, (t0, n) in enumerate(tiles):
                    for dc in range(NC_):
                        bk = 4 + cnt["dn"] % 2
                        cnt["dn"] += 1

                        def fd(pe, bk=bk, dc=dc, t0=t0, n=n):
                            ins = None
                            q_ = 0
                            for (wdt, wdb, par) in wds:
                                for fc in range(4):
                                    ins = pe.matmul(banks[bk][:, 0:n], wdt[:, fc, dc * 128:(dc + 1) * 128],
                                                    actb[par][:, fc, t0:t0 + n], start=(q_ == 0), stop=(q_ == nmm - 1))
                                    q_ += 1
                            return ins
                        rd = [w[1] for w in wds] + [B_act[w[2]][fc][ti] for w in wds for fc in range(4)]
                        op("pe", fd, reads=rd, writes=[B_pb[bk]])
                        op("dve", lambda v, bk=bk, dc=dc, t0=t0, n=n: v.scalar_tensor_tensor(
                            out=hT[:, dc, t0:t0 + n], in0=banks[bk][:, 0:n], scalar=0.5, in1=hT[:, dc, t0:t0 + n],
                            op0=ALU.mult, op1=ALU.add),
                           reads=[B_pb[bk], B_hT[(dc, ti)]], writes=[B_hT[(dc, ti)]])

            g = 0
            while g < NG:
                grp = [g] if g == NG - 1 else [g, g + 1]
                for gg in grp:
                    gate_up(gg)
                down(grp)
                g += len(grp)
            tr.barrier()

    def win_phase(tiles, own, QT, B_QT, XPt, B_XP, prevX, prevY, B_prev):
        T = T_OWN if own else T_OTH
        loc0 = 1024 if own else 0
        with ExitStack() as st:
            wp_ = WPool("wi", 3, (128, NC_, 256), st)
            kst = [sb("kst%d" % i, (128, 512), BF16, st) for i in range(2)]
            B_kst = [Buf(), Buf()]
            NST = 6
            tst = [sb("tst%d" % i, (128, 256), F32, st) for i in range(NST)]
            B_tst = [Buf() for _ in range(NST)]
            tsb = [sb("tsb%d" % i, (128, 256), BF16, st) for i in range(NST)]
            B_tsb = [Buf() for _ in range(NST)]
            c = {"a": 0, "k": 0, "t": 0}
            units = list(range(16)) if own else list(range(4, 16))
            for u in units:
                wt, wb = wp_.load(wcols(w_in, u * 256, 256))
                kind = u // 4
                if kind in (0, 1, 3):
                    for j in range(2):
                        ch = (u % 4) * 2 + j
                        if kind == 3 and not own:
                            for which, t0 in ((0, 497), (1, 1009)):
                                bk = c["a"] % 2
                                c["a"] += 1

                                def f3(pe, bk=bk, j=j, t0=t0):
                                    ins = None
                                    for k in range(NC_):
                                        ins = pe.matmul(banks[bk][:, 0:15], wt[:, k, j * 128:(j + 1) * 128],
                                                        U[:, k, t0:t0 + 15], start=(k == 0), stop=(k == NC_ - 1))
                                    return ins
                                op("pe", f3, reads=[wb, B_U[which]], writes=[B_pb[bk]])
                                dstp = (prevX if which == 0 else prevY)
                                op("act", lambda a, dstp=dstp, ch=ch, bk=bk: a.copy(dstp[:, ch, :], banks[bk][:, 0:15]),
                                   reads=[B_pb[bk]], writes=[B_prev])
                            continue
                        for ti, (t0, n) in enumerate(tiles):
                            bk = c["a"] % 2
                            c["a"] += 1

                            def ff(pe, bk=bk, j=j, t0=t0, n=n):
                                ins = None
                                for k in range(NC_):
                                    ins = pe.matmul(banks[bk][:, 0:n], wt[:, k, j * 128:(j + 1) * 128],
                                                    U[:, k, t0:t0 + n], start=(k == 0), stop=(k == NC_ - 1))
                                return ins
                            op("pe", ff, reads=[wb, B_U[ti]], writes=[B_pb[bk]])
                            if kind == 0:
                                op("act", lambda a, ch=ch, bk=bk, t0=t0, n=n: a.mul(QT[:, ch, t0:t0 + n], banks[bk][:, 0:n], SB_SCALE),
                                   reads=[B_pb[bk]], writes=[B_QT])
                            elif kind == 1:
                                ks = c["k"] % 2
                                c["k"] += 1
                                op("act", lambda a, ks=ks, bk=bk, n=n: a.copy(kst[ks][:, 0:n], banks[bk][:, 0:n]),
                                   reads=[B_pb[bk]], writes=[B_kst[ks]])
                                dma("sp", kT_scr[ch, :, loc0 + t0:loc0 + t0 + n], kst[ks][:, 0:n],
                                    reads=[B_kst[ks]], writes=[B_kscr])
                            else:
                                op("act", lambda a, ch=ch, bk=bk, ti=ti, n=n: a.copy(XPt[ti][:, ch, 15:15 + n], banks[bk][:, 0:n]),
                                   reads=[B_pb[bk]], writes=[B_XP[ti]])
                if (kind == 1 and own) or kind == 2:
                    col0 = (u % 4) * 256
                    nblk = (T + 127) // 128
                    for b in range(nblk):
                        nb = min(128, T - b * 128)
                        ti = min(b * 128 // 512, 2)
                        bk = 2 + c["t"] % 2
                        c["t"] += 1

                        def ft(pe, bk=bk, b=b, nb=nb):
                            ins = None
                            for k in range(NC_):
                                ins = pe.matmul(banks[bk][0:nb, 0:256], U[:, k, b * 128:b * 128 + nb], wt[:, k, :],
                                                start=(k == 0), stop=(k == NC_ - 1))
                            return ins
                        op("pe", ft, reads=[wb, B_U[ti]], writes=[B_pb[bk]])
                        s = c["t"] % NST
                        if own:
                            op("dve", lambda v, s=s, bk=bk, nb=nb: v.tensor_copy(tst[s][0:nb, :], banks[bk][0:nb, 0:256]),
                               reads=[B_pb[bk]], writes=[B_tst[s]])
                            dstap = (k_own if kind == 1 else v_own)[b * 128:b * 128 + nb, col0:col0 + 256]
                            dma("sp", dstap, tst[s][0:nb, :], reads=[B_tst[s]])
                        if kind == 2:
                            if own:
                                op("act", lambda a, s=s, nb=nb: a.copy(tsb[s][0:nb, :], tst[s][0:nb, :]),
                                   reads=[B_tst[s]], writes=[B_tsb[s]])
                            else:
                                op("act", lambda a, s=s, bk=bk, nb=nb: a.copy(tsb[s][0:nb, :], banks[bk][0:nb, 0:256]),
                                   reads=[B_pb[bk]], writes=[B_tsb[s]])
                            r0 = loc0 + b * 128
                            dma("sp", v_scr[r0:r0 + nb, col0:col0 + 256], tsb[s][0:nb, :],
                                reads=[B_tsb[s]], writes=[B_vscr])
            tr.barrier()

    try:
        stop(1)
        prevX = sb("prevX", (128, 8, 15), F32)
        prevY = sb("prevY", (128, 8, 15), F32)
        B_prev = Buf()

        with ExitStack() as st:
            load_xT(x_oth, T_OTH, st)
            tr.barrier()
        stop(2)
        norm_h_to_U(G_FFN1, TILES_OTH)
        stop(3)
        ffn(w1g, w1u, w1d, TILES_OTH)
        stop(4)
        norm_h_to_U(G_MIX, TILES_OTH)
        win_phase(TILES_OTH, False, None, None, None, None, prevX, prevY, B_prev)
        stop(5)

        with ExitStack() as st:
            load_xT(x_own, T_OWN, st)
            tr.barrier()
        norm_h_to_U(G_FFN1, TILES_FFN)
        ffn(w1g, w1u, w1d, TILES_FFN)
        norm_h_to_U(G_MIX, TILES_OWN)
        stop(7)

        with ExitStack() as mixst:
            QT = sb("QT", (128, 8, T_OWN), BF16, mixst)
            B_QT = Buf()
            xpst = ExitStack()
            XPt = [sb("XP%d" % i, (128, 8, 15 + n), F32, xpst) for i, (t0, n) in enumerate(TILES_OWN)]
            B_XP = [Buf() for _ in range(3)]
            win_phase(TILES_OWN, True, QT, B_QT, XPt, B_XP, prevX, prevY, B_prev)
            stop(8)

            with ExitStack() as st:
                invt = sb("invt", (128, 4, 256), F32, st)
                B_inv = Buf()
                wpl = sb("wpl", (128, 4, 2, 256), BF16, st)
                B_wpl = Buf()
                dma("pool", wpl[:], w_pool.rearrange("g (cc p) d -> p g cc d", p=128), writes=[B_wpl])
                pa = sb("pa", (128, 2, 271), F32, st)
                pbuf = sb("pbuf", (128, 2, 271), F32, st)
                B_pa, B_pbb = Buf(), Buf()
                pa2 = sb("pa2", (128, 2, 271), F32, st)
                pbuf2 = sb("pbuf2", (128, 2, 271), F32, st)
                B_pa2, B_pbb2 = Buf(), Buf()
                pooled = sb("pooled", (128, 8, 256), BF16, st)
                B_pooled = [Buf() for _ in range(4)]
                mxt = sb("mxt", (128, 8, 256), F32, st)
                B_mxt = [Buf() for _ in range(8)]
                B_spt = Buf()
                dma("sp", mxt[0:15, 0:4, :], spool.rearrange("r (a b) -> r a b", a=4), writes=[B_spt])
                op("dve", lambda v: v.tensor_scalar(out=XPt[0][:, :, 0:15], in0=prevY[:], scalar1=flag[:, 0:1], scalar2=None,
                                                    op0=ALU.mult), reads=[B_prev, B_c], writes=[B_XP[0]])
                op("dve", lambda v: v.tensor_copy(XPt[1][:, :, 0:15], prevX[:]), reads=[B_prev], writes=[B_XP[1]])
                for half in range(2):
                    def fsp(pe, half=half):
                        ins = None
                        for j in range(4):
                            cch = half * 4 + j
                            ins = pe.transpose(banks[half][:, j * 128:j * 128 + 15],
                                               mxt[0:15, cch // 2, (cch % 2) * 128:(cch % 2) * 128 + 128],
                                               ident_f[0:15, 0:15])
                        return ins
                    op("pe", fsp, reads=[B_spt, B_c], writes=[B_pb[half]])
                    op("act", lambda a, half=half: a.copy(
                        XPt[2][:, half * 4:half * 4 + 4, 0:15],
                        banks[half][:].rearrange("p (j t) -> p j t", j=4)[:, :, 0:15]),
                       reads=[B_pb[half]], writes=[B_XP[2]])
                tr.barrier()

                subtiles = [(0, 0, 256), (0, 256, 256), (1, 0, 256), (1, 256, 256), (2, 0, 32)]
                for ti, off, n in subtiles:
                    t0 = TILES_OWN[ti][0] + off
                    L = 15 + n
                    X = XPt[ti]
                    dma("sp", invt[:, :, 0:n], invc[:, :, t0:t0 + n], writes=[B_inv])
                    for g in range(4):
                        src_t, src_b = None, B_XP[ti]
                        pe_ = "pool" if g == 3 else "dve"
                        bufs = [(pa2, B_pa2), (pbuf2, B_pbb2)] if g == 3 else [(pa, B_pa), (pbuf, B_pbb)]
                        for s_ in range(g + 1):
                            sh = 1 << s_
                            lo = (1 << (s_ + 1)) - 1
                            dt_, db_ = bufs[s_ % 2]
                            if s_ == 0:
                                a0 = X[:, 2 * g:2 * g + 2, off + lo:off + L]
                                a1 = X[:, 2 * g:2 * g + 2, off + lo - sh:off + L - sh]
                            else:
                                a0 = src_t[:, :, lo:L]
                                a1 = src_t[:, :, lo - sh:L - sh]
                            op(pe_, lambda v, dt_=dt_, a0=a0, a1=a1, lo=lo, L=L: v.tensor_tensor(
                                out=dt_[:, :, lo:L], in0=a0, in1=a1, op=ALU.add), reads=[src_b], writes=[db_])
                            src_t, src_b = dt_, db_
                        ot, ob = bufs[(g + 1) % 2]
                        for cc in range(2):
                            op(pe_, lambda v, cc=cc, ot=ot, g=g, src_t=src_t, L=L, n=n: v.tensor_tensor(
                                out=ot[:, cc, 15:L], in0=src_t[:, cc, 15:L], in1=invt[:, g, 0:n], op=ALU.mult),
                               reads=[src_b, B_inv], writes=[ob])
                        op(pe_, lambda v, ot=ot, g=g, L=L, n=n, X=X, off=off: v.tensor_tensor(
                            out=pooled[:, 2 * g:2 * g + 2, 0:n], in0=ot[:, :, 15:L], in1=X[:, 2 * g:2 * g + 2, off + 15:off + L],
                            op=ALU.subtract), reads=[ob, B_XP[ti]], writes=[B_pooled[g]])
                    for g in range(4):
                        for j in range(2):
                            bk = (g * 2 + j) % 2

                            def fm(pe, g=g, j=j, bk=bk, n=n):
                                ins = None
                                for cc in range(2):
                                    ins = pe.matmul(banks[bk][:, 0:n], wpl[:, g, cc, j * 128:(j + 1) * 128],
                                                    pooled[:, 2 * g + cc, 0:n], start=(cc == 0), stop=(cc == 1))
                                return ins
                            op("pe", fm, reads=[B_wpl, B_pooled[g]], writes=[B_pb[bk]])
                            op("act", lambda a, g=g, j=j, bk=bk, n=n: a.copy(mxt[:, 2 * g + j, 0:n], banks[bk][:, 0:n]),
                               reads=[B_pb[bk]], writes=[B_mxt[2 * g + j]])
                    rmsnorm(mxt, lambda c, _ti: B_mxt[c], 8, G_POOL,
                            lambda c, _t0, n_, t0=t0: U[:, 8 + c, t0:t0 + n_], lambda c, _ti, ti=ti: B_U[ti], [(0, n)], 1024)
                tr.barrier()

                for which, (X, bx, oap, nlast) in enumerate(((XPt[1], B_XP[1], pool_p, 512), (XPt[2], B_XP[2], pool_s, 32))):
                    for half in range(2):
                        def fpo(pe, half=half, X=X, nlast=nlast):
                            ins = None
                            for j in range(4):
                                cch = half * 4 + j
                                ins = pe.transpose(banks[2 + half][0:15, j * 128:(j + 1) * 128], X[:, cch, nlast:nlast + 15],
                                                   ident_f[:, :])
                            return ins
                        op("pe", fpo, reads=[bx, B_c], writes=[B_pb[2 + half]])
                        op("act", lambda a, half=half: a.copy(
                            mxt[0:15, half * 2:half * 2 + 2, :],
                            banks[2 + half][0:15, :].rearrange("p (a b) -> p a b", a=2)),
                           reads=[B_pb[2 + half]], writes=[B_spt])
                    dma("sp", oap.rearrange("r (a b) -> r a b", a=4), mxt[0:15, 0:4, :], reads=[B_spt], writes=[B_spt])
                tr.barrier()
            xpst.close()
            stop(9)

            with ExitStack() as st:
                ATT = sb("ATT", (128, 8, T_OWN), F32, st)
                B_ATT = [Buf() for _ in range(8)]
                dmk = sb("dmk", (128, 4, 512), BF16, st)
                B_dmk = Buf()
                dma("pool", dmk[:], dmask, writes=[B_dmk])
                KT = sb("KT", (128, 2048), BF16, st)
                B_KT = [Buf() for _ in range(4)]
                VV = sb("VV", (128, 16, 128), BF16, st)
                B_VV = [Buf() for _ in range(4)]
                KTs = sb("KTs", (128, 32), BF16, st)
                VVs = sb("VVs", (32, 128), BF16, st)
                B_KVs = Buf()
                CKr = sb("CKr", (128, 16, 128), BF16, st)
                B_CKr = Buf()
                CKT = sb("CKT", (128, 2048), BF16, st)
                B_CKT = Buf()
                CV = sb("CV", (128, 16, 128), BF16, st)
                B_CV = Buf()
                widths = [32, 512, 512]
                e2 = [[sb("e2_%d_%d" % (s_, i), (128, widths[s_]), F32, st) for i in range(2)] for s_ in range(3)]
                lp = [[sb("lp_%d_%d" % (s_, i), (128, widths[s_]), BF16, st) for i in range(2)] for s_ in range(3)]
                ab = [[sb("ab_%d_%d" % (s_, i), (128, widths[s_]), BF16, st) for i in range(2)] for s_ in range(3)]
                Rb = [sb("R%d" % s_, (128, widths[s_]), BF16, st) for s_ in range(3)]
                B_e2 = [[Buf(), Buf()] for _ in range(3)]
                B_lp = [[Buf(), Buf()] for _ in range(3)]
                B_ab = [[Buf(), Buf()] for _ in range(3)]
                B_R = [Buf() for _ in range(3)]
                ctr = {"z": 0, "s": 0}

                def nxt(key, n):
                    v = ctr[key] % n
                    ctr[key] += 1
                    return v

                def load_prompt(h):
                    for tl in (3, 0, 2, 1):
                        c0 = tl * 512
                        dma("sp", KT[:, c0:c0 + 512], kT_scr[h, :, c0:c0 + 512], reads=[B_kscr], writes=[B_KT[tl]])
                        dma("sp", VV[:, tl * 4:tl * 4 + 4, :],
                            v_scr[c0:c0 + 512, h * 128:(h + 1) * 128].rearrange("(b p) d -> p b d", p=128),
                            reads=[B_vscr], writes=[B_VV[tl]])

                def load_sample(h):
                    dma("sp", KTs[:], kT_scr[h, :, 2048:2080], reads=[B_kscr], writes=[B_KVs])
                    dma("sp", VVs[:], v_scr[2048:2080, h * 128:(h + 1) * 128], reads=[B_vscr], writes=[B_KVs])
                    dma("pool", CKr[:], ck[:, h * 128:(h + 1) * 128].rearrange("(b p) d -> p b d", p=128), writes=[B_CKr])
                    dma("pool", CV[:], cv[:, h * 128:(h + 1) * 128].rearrange("(b p) d -> p b d", p=128), writes=[B_CV])
                    for half in range(2):
                        def ftk(pe, half=half):
                            ins = None
                            for j in range(8):
                                b_ = half * 8 + j
                                ins = pe.transpose(pb6_bf[:, j * 128:(j + 1) * 128], CKr[:, b_, :], ident_b)
                            return ins
                        op("pe", ftk, reads=[B_CKr, B_c], writes=[B_pb[6]])
                        op("dve", lambda v, half=half: v.tensor_copy(CKT[:, half * 1024:(half + 1) * 1024], pb6_bf[:, :]),
                           reads=[B_pb[6]], writes=[B_CKT])

                def job_gen(h, q0, nq, steps, ob, sid):
                    ns = len(steps)
                    qap = QT[:, h, q0:q0 + nq]
                    R, bR = Rb[sid], B_R[sid]

                    def mask_mul(buf, bbuf, nk, mask):
                        if mask is None:
                            return
                        if mask[0] == "diag":
                            mi = mask[1]
                            op("dve", lambda v: v.tensor_tensor(out=buf[0:nk, 0:nq], in0=buf[0:nk, 0:nq],
                                                                in1=dmk[0:nk, mi, 0:nq], op=ALU.mult),
                               reads=[bbuf, B_dmk], writes=[bbuf])
                        else:
                            op("dve", lambda v: v.tensor_scalar(out=buf[0:nk, 0:nq], in0=buf[0:nk, 0:nq],
                                                                scalar1=flag[0:nk, 0:1], scalar2=None, op0=ALU.mult),
                               reads=[bbuf, B_c], writes=[bbuf])

                    def st_z(i):
                        kap, kbuf, vap, vbuf, nk, mask = steps[i]
                        zb = nxt("z", 2)
                        op("pe", lambda pe: pe.matmul(banks[zb][0:nk, 0:nq], kap, qap, start=True, stop=True),
                           reads=[kbuf, B_QT], writes=[B_pb[zb]])
                        return zb

                    def st_e(i, zb):
                        kap, kbuf, vap, vbuf, nk, mask = steps[i]
                        e, be = e2[sid][i % 2], B_e2[sid][i % 2]
                        op("act", lambda a: a.activation(e[0:nk, 0:nq], banks[zb][0:nk, 0:nq], AF.Exp),
                           reads=[B_pb[zb]], writes=[be])
                        mask_mul(e, be, nk, mask)

                    def st_l(i):
                        kap, kbuf, vap, vbuf, nk, mask = steps[i]
                        op("act", lambda a: a.activation(lp[sid][i % 2][0:nk, 0:nq], e2[sid][i % 2][0:nk, 0:nq], AF.Ln,
                                                         bias=1.0, scale=1.0),
                           reads=[B_e2[sid][i % 2]], writes=[B_lp[sid][i % 2]])

                    def st_s(i):
                        kap, kbuf, vap, vbuf, nk, mask = steps[i]
                        l, bl = lp[sid][i % 2], B_lp[sid][i % 2]
                        a_, ba = ab[sid][i % 2], B_ab[sid][i % 2]
                        sbk = 2 + nxt("s", 2)

                        def fs(pe):
                            pe.matmul(banks[sbk][0:nk, 0:nq], ntri_b[0:nk, 0:nk], l[0:nk, 0:nq], start=True, stop=False)
                            if i > 0:
                                pe.matmul(banks[sbk][0:nk, 0:nq], nones_b[:, 0:nk], R[:, 0:nq], start=False, stop=False)
                            return pe.matmul(banks[sbk][0:nk, 0:nq], kap, qap, start=False, stop=True)
                        op("pe", fs, reads=[bl, B_c, kbuf, B_QT] + ([bR] if i > 0 else []), writes=[B_pb[sbk]])
                        if i < ns - 1:
                            if i == 0:
                                if nk < 128:
                                    op("pool", lambda g: g.memset(R[:, 0:nq], 0.0), writes=[bR])
                                op("pool", lambda g: g.tensor_copy(R[0:nk, 0:nq], l[0:nk, 0:nq]), reads=[bl], writes=[bR])
                            else:
                                op("pool", lambda g: g.tensor_tensor(out=R[0:nk, 0:nq], in0=R[0:nk, 0:nq], in1=l[0:nk, 0:nq],
                                                                     op=ALU.add), reads=[bl, bR], writes=[bR])
                        op("act", lambda a: a.activation(a_[0:nk, 0:nq], banks[sbk][0:nk, 0:nq], AF.Exp),
                           reads=[B_pb[sbk]], writes=[ba])
                        mask_mul(a_, ba, nk, mask)

                    def st_v(i):
                        kap, kbuf, vap, vbuf, nk, mask = steps[i]
                        op("pe", lambda pe: pe.matmul(banks[ob][:, 0:nq], vap, ab[sid][i % 2][0:nk, 0:nq],
                                                      start=(i == 0), stop=(i == ns - 1)),
                           reads=[vbuf, B_ab[sid][i % 2]], writes=[B_pb[ob]])

                    for t in range(ns + 3):
                        zb = st_z(t) if t < ns else None
                        if 0 <= t - 1 < ns:
                            st_l(t - 1)
                        if t < ns:
                            st_e(t, zb)
                        if 0 <= t - 2 < ns:
                            st_s(t - 2)
                        if 0 <= t - 3 < ns:
                            st_v(t - 3)
                        yield
                    op("act", lambda a: a.copy(ATT[:, h, q0:q0 + nq], banks[ob][:, 0:nq]), reads=[B_pb[ob]], writes=[B_ATT[h]])

                def tile_steps(loc, mask_kind):
                    out = []
                    for kb in (3, 2, 1, 0):
                        t0 = loc + kb * 128
                        if mask_kind == "diag":
                            m = ("diag", kb)
                        elif mask_kind == "flag":
                            m = ("flag",)
                        else:
                            m = None
                        out.append((KT[:, t0:t0 + 128], B_KT[loc // 512], VV[:, t0 // 128, :], B_VV[loc // 512], 128, m))
                    return out

                LOX, LOY, LS0, LS1 = 0, 512, 1024, 1536
                for h in range(8):
                    load_sample(h)
                    load_prompt(h)
                    st_s = [(KTs[:, :], B_KVs, VVs[:, :], B_KVs, 32, ("diag", 0))]
                    for b_ in range(15, -1, -1):
                        st_s.append((CKT[:, b_ * 128:(b_ + 1) * 128], B_CKT, CV[:, b_, :], B_CV, 128, None))
                    gens = [
                        job_gen(h, 1024, 32, st_s, 4, 0),
                        job_gen(h, 512, 512, tile_steps(LS1, "diag") + tile_steps(LOX, None) + tile_steps(LS0, None)
                                + tile_steps(LOY, "flag"), 5, 1),
                        job_gen(h, 0, 512, tile_steps(LS0, "diag") + tile_steps(LOY, "flag"), 7, 2),
                    ]
                    while gens:
                        for g_ in list(gens):
                            try:
                                next(g_)
                            except StopIteration:
                                gens.remove(g_)

                rmsnorm(ATT, lambda c, ti: B_ATT[c], 8, G_ATT, lambda c, t0, n: U[:, c, t0:t0 + n],
                        lambda c, ti: B_U[ti], TILES_OWN, 1024)
                tr.barrier()

            stop(10)
            with ExitStack() as st:
                wo = WPool("wo", 3, (128, NC_, 256), st)
                cn = 0
                for u in range(8):
                    wt, wb = wo.load(wcols(w_out, u * 256, 256))
                    for ti, (t0, n) in enumerate(TILES_OWN):
                        for j in range(2):
                            dc = u * 2 + j
                            bk = cn % 2
                            cn += 1

                            def fo(pe, bk=bk, j=j, t0=t0, n=n, wt=wt):
                                ins = None
                                for k in range(NC_):
                                    ins = pe.matmul(banks[bk][:, 0:n], wt[:, k, j * 128:(j + 1) * 128], U[:, k, t0:t0 + n],
                                                    start=(k == 0), stop=(k == NC_ - 1))
                                return ins
                            op("pe", fo, reads=[wb, B_U[ti]], writes=[B_pb[bk]])
                            op("dve", lambda v, bk=bk, dc=dc, t0=t0, n=n: v.tensor_tensor(
                                out=hT[:, dc, t0:t0 + n], in0=hT[:, dc, t0:t0 + n], in1=banks[bk][:, 0:n], op=ALU.add),
                               reads=[B_pb[bk], B_hT[(dc, ti)]], writes=[B_hT[(dc, ti)]])
                tr.barrier()
        tr.barrier()

        stop(11)
        norm_h_to_U(G_FFN2, TILES_FFN)
        ffn(w2g, w2u, w2d, TILES_FFN)

        stop(12)
        norm_h_to_U(G_PLE, TILES_FFN)
        with ExitStack() as st:
            PT = sb("PT", (128, 2, T_OWN), BF16, st)
            B_PT = Buf()
            pin = [sb("pin%d" % i, (128, PLE), F32, st) for i in range(2)]
            B_pin = [Buf(), Buf()]
            wppt = sb("wppt", (128, 2, D), BF16, st)
            B_wpp = Buf()
            dma("pool", wppt[:], wpp.rearrange("(c p) f -> p c f", p=128), writes=[B_wpp])
            nblk = (T_OWN + 127) // 128
            for b in range(nblk):
                nb = min(128, T_OWN - b * 128)
                i = b % 2
                dma("sp", pin[i][0:nb, :], p_own[b * 128:b * 128 + nb, :], writes=[B_pin[i]])

                def fp(pe, i=i, nb=nb):
                    ins = None
                    for j in range(2):
                        ins = pe.transpose(banks[i][:, j * 128:j * 128 + nb], pin[i][0:nb, j * 128:(j + 1) * 128], ident_f[0:nb, 0:nb])
                    return ins
                op("pe", fp, reads=[B_pin[i], B_c], writes=[B_pb[i]])
                op("act", lambda a, i=i, b=b, nb=nb: a.copy(
                    PT[:, :, b * 128:b * 128 + nb], banks[i][:, 0:256].rearrange("p (j t) -> p j t", j=2)[:, :, 0:nb]),
                   reads=[B_pb[i]], writes=[B_PT])
            wg_ = WPool("wpg", 3, (128, NC_, 256), st)
            sig = [sb("sig%d" % i, (128, 512), F32, st) for i in range(2)]
            B_sig = [Buf(), Buf()]
            cn = 0
            for u in range(8):
                wt, wb = wg_.load(wcols(wpg, u * 256, 256))
                for ti, (t0, n) in enumerate(TILES_FFN):
                    for j in range(2):
                        dc = u * 2 + j
                        gb = cn % 2
                        pbk = 2 + cn % 2
                        cn += 1

                        def fgate(pe, gb=gb, j=j, t0=t0, n=n, wt=wt):
                            ins = None
                            for k in range(NC_):
                                ins = pe.matmul(banks[gb][:, 0:n], wt[:, k, j * 128:(j + 1) * 128], U[:, k, t0:t0 + n],
                                                start=(k == 0), stop=(k == NC_ - 1))
                            return ins
                        op("pe", fgate, reads=[wb, B_U[ti]], writes=[B_pb[gb]])

                        def fproj(pe, pbk=pbk, dc=dc, t0=t0, n=n):
                            ins = None
                            for k in range(2):
                                ins = pe.matmul(banks[pbk][:, 0:n], wppt[:, k, dc * 128:(dc + 1) * 128], PT[:, k, t0:t0 + n],
                                                start=(k == 0), stop=(k == 1))
                            return ins
                        op("pe", fproj, reads=[B_wpp, B_PT], writes=[B_pb[pbk]])
                        s = cn % 2
                        op("act", lambda a, s=s, gb=gb, n=n: a.activation(sig[s][:, 0:n], banks[gb][:, 0:n], AF.Sigmoid),
                           reads=[B_pb[gb]], writes=[B_sig[s]])
                        op("dve", lambda v, s=s, pbk=pbk, n=n: v.tensor_tensor(out=sig[s][:, 0:n], in0=sig[s][:, 0:n],
                                                                              in1=banks[pbk][:, 0:n], op=ALU.mult),
                           reads=[B_sig[s], B_pb[pbk]], writes=[B_sig[s]])
                        op("dve", lambda v, s=s, dc=dc, t0=t0, n=n: v.tensor_tensor(out=hT[:, dc, t0:t0 + n], in0=hT[:, dc, t0:t0 + n],
                                                                                    in1=sig[s][:, 0:n], op=ALU.add),
                           reads=[B_sig[s], B_hT[(dc, ti)]], writes=[B_hT[(dc, ti)]])
            tr.barrier()

        stop(13)
        rmsnorm(hT, hbuf, NC_, G_FIN, lambda c, t0, n: hT[:, c, t0:t0 + n], hbuf, TILES_OWN, D)
        with ExitStack() as st:
            yo = [sb("yo%d" % i, (128, D), F32, st) for i in range(3)]
            B_yo = [Buf(), Buf(), Buf()]
            nblk = (T_OWN + 127) // 128
            q = 0
            for b in range(nblk):
                nb = min(128, T_OWN - b * 128)
                ti = min(b * 128 // 512, 2)
                i = b % 3
                for cg in range(4):
                    bk = q % 2
                    q += 1

                    def fy(pe, bk=bk, cg=cg, b=b, nb=nb):
                        ins = None
                        for j in range(4):
                            c = cg * 4 + j
                            ins = pe.transpose(banks[bk][0:nb, j * 128:(j + 1) * 128], hT[:, c, b * 128:b * 128 + nb], ident_f[:, :])
                        return ins
                    op("pe", fy, reads=[B_hT[(cg * 4 + j, ti)] for j in range(4)] + [B_c], writes=[B_pb[bk]])
                    if q % 2:
                        op("act", lambda a, i=i, bk=bk, cg=cg, nb=nb: a.copy(yo[i][0:nb, cg * 512:(cg + 1) * 512], banks[bk][0:nb, :]),
                           reads=[B_pb[bk]], writes=[B_yo[i]])
                    else:
                        op("dve", lambda v, i=i, bk=bk, cg=cg, nb=nb: v.tensor_copy(yo[i][0:nb, cg * 512:(cg + 1) * 512], banks[bk][0:nb, :]),
                           reads=[B_pb[bk]], writes=[B_yo[i]])
                dma("sp", y_own[b * 128:b * 128 + nb, :], yo[i][0:nb, :], reads=[B_yo[i]])
            tr.finish()

    except _Stop:
        tr.finish()
    return nc


_CACHE = {}


def _consts():
    cm = np.zeros((128, 4, 128), np.float32)
    cm[:, 0, :] = np.eye(128, dtype=np.float32)
    cm[:, 1, :] = 1.0
    j = np.arange(128)[:, None]
    k = np.arange(128)[None, :]
    cm[:, 2, :] = np.where(j >= k, -1.0, 0.0)
    cm[:, 3, :] = -1.0
    dm = np.zeros((128, 4, 512), np.float32)
    q = np.arange(512)[None, :]
    for i in range(4):
        dm[:, i, :] = ((128 * i + np.arange(128)[:, None]) < q).astype(np.float32)
    return cm, dm


def kernel(x_prompt, x_sample, cache_k, cache_v, state_pool, p_prompt, p_sample,
           g_ffn1, w1_gate, w1_up, w1_down, g_mix, w_in, g_attn_out, w_pool, pool_scale,
           w_out, g_ffn2, w2_gate, w2_up, w2_down, g_ple, w_ple_gate, w_ple_proj, g_final):
    f = lambda a: np.ascontiguousarray(np.asarray(a, dtype=np.float32))
    x_prompt, x_sample, cache_k, cache_v, state_pool, p_prompt, p_sample = map(
        f, (x_prompt, x_sample, cache_k, cache_v, state_pool, p_prompt, p_sample))
    if "nc" not in _CACHE:
        _CACHE["nc"] = build_program()
    nc = _CACHE["nc"]

    def g16(g):
        return f(g).reshape(-1, 128).T
    gains = np.ascontiguousarray(np.concatenate(
        [g16(g_ffn1[0]), g16(g_mix[0]), g16(g_ffn2[0]), g16(g_ple[0]), g16(g_final), g16(g_attn_out[0]),
         g16(pool_scale[0])], axis=1))
    cm, dm = _consts()
    shared = {
        "w1g": f(w1_gate[0]), "w1u": f(w1_up[0]), "w1d": f(w1_down[0]),
        "w2g": f(w2_gate[0]), "w2u": f(w2_up[0]), "w2d": f(w2_down[0]),
        "w_in": f(w_in[0]), "w_pool": f(w_pool[0]), "w_out": f(w_out[0]),
        "wpg": f(w_ple_gate[0]), "wpp": f(w_ple_proj[0]), "gains": gains, "cmat": cm, "dmask": dm,
    }
    wins = np.array([2, 4, 8, 16], np.float32)
    own_tiles = {0: (0, 2), 1: (1, 3)}
    oth_tiles = {0: (1, 3), 1: (2, 0)}
    in_maps = []
    for c in range(8):
        p, r = c // 2, c % 2
        s0, s1 = own_tiles[r]
        ox, oy = oth_tiles[r]
        tl = lambda a, g: a[p, g * 512:(g + 1) * 512]
        pos = np.concatenate([np.arange(s0 * 512, s0 * 512 + 512), np.arange(s1 * 512, s1 * 512 + 512),
                              2048 + np.arange(32)]).astype(np.float32)
        cnt = np.minimum(pos[None, :] + 1.0, wins[:, None])
        invc = np.ascontiguousarray(np.broadcast_to((1.0 / cnt)[None], (128, 4, T_OWN))).astype(np.float32)
        m = dict(shared)
        m.update({
            "x_own": np.ascontiguousarray(np.concatenate([tl(x_prompt, s0), tl(x_prompt, s1), x_sample[c]], axis=0)),
            "x_oth": np.ascontiguousarray(np.concatenate([tl(x_prompt, ox), tl(x_prompt, oy)], axis=0)),
            "p_own": np.ascontiguousarray(np.concatenate([tl(p_prompt[0], s0), tl(p_prompt[0], s1), p_sample[0, c]], axis=0)),
            "ck": np.ascontiguousarray(cache_k[0, c].reshape(2048, 1024)),
            "cv": np.ascontiguousarray(cache_v[0, c].reshape(2048, 1024)),
            "spool": np.ascontiguousarray(state_pool[0, c]),
            "flagv": np.full((128, 1), float(r), np.float32),
            "invc": invc,
        })
        in_maps.append(m)
    res = run_bass_kernel_spmd(nc, in_maps, core_ids=list(range(8)))
    R = res.results
    y_prompt = np.zeros((4, 2048, D), np.float32)
    y_sample = np.zeros((8, 32, D), np.float32)
    nkp = np.zeros((1, 4, 2048, 8, 128), np.float32)
    nvp = np.zeros((1, 4, 2048, 8, 128), np.float32)
    npp = np.zeros((1, 4, 15, 1024), np.float32)
    nks = np.zeros((1, 8, 32, 8, 128), np.float32)
    nvs = np.zeros((1, 8, 32, 8, 128), np.float32)
    nps = np.zeros((1, 8, 15, 1024), np.float32)
    for c in range(8):
        p, r = c // 2, c % 2
        o = R[c]
        for slot, g in enumerate(own_tiles[r]):
            sl = slice(g * 512, (g + 1) * 512)
            y_prompt[p, sl] = o["y_own"][slot * 512:(slot + 1) * 512]
            nkp[0, p, sl] = o["k_own"][slot * 512:(slot + 1) * 512].reshape(512, 8, 128)
            nvp[0, p, sl] = o["v_own"][slot * 512:(slot + 1) * 512].reshape(512, 8, 128)
        y_sample[c] = o["y_own"][1024:1056]
        nks[0, c] = o["k_own"][1024:1056].reshape(32, 8, 128)
        nvs[0, c] = o["v_own"][1024:1056].reshape(32, 8, 128)
        nps[0, c] = o["pool_s"]
        if r == 1:
            npp[0, p] = o["pool_p"]
    return (y_prompt, y_sample, nkp, nvp, npp, nks, nvs, nps)
```
